# Optimizing a Trainium2 kernel written in Bass

```python
import jax, jax.numpy as jnp
from jax import lax
import numpy as np

D_MODEL = 1024
BATCH = 2
SEQ = 8192
DEPTH = 1

DN_HEADS = 4
DN_HEAD_DIM = 128
DN_WIDTH = DN_HEADS * DN_HEAD_DIM
DN_CONV = 5
DN_CHUNK = 64
N_DIR = 2
SWA_Q_HEADS = 8
SWA_KV_HEADS = 2
SWA_HEAD_DIM = 64
SWA_WIDTH = SWA_Q_HEADS * SWA_HEAD_DIM
SWA_KV_WIDTH = SWA_KV_HEADS * SWA_HEAD_DIM
WINDOW = 128
BLOCK = 128
ROPE_THETA = 10000.0
EPS = 1e-6
MIX_WIDTH = DN_WIDTH + SWA_WIDTH
SPLITS = (DN_WIDTH, DN_WIDTH, DN_WIDTH, DN_WIDTH, N_DIR * DN_HEADS, N_DIR * DN_HEADS,
          SWA_WIDTH, SWA_KV_WIDTH, SWA_KV_WIDTH, SWA_WIDTH)
IN_WIDTH = 4 * DN_WIDTH + 2 * N_DIR * DN_HEADS + 2 * SWA_WIDTH + 2 * SWA_KV_WIDTH

kernel_name = 'hybrid_gdn_swa_parallel_heads'


def rms_norm(x, w):
    xf = x.astype(jnp.float32)
    y = xf * lax.rsqrt(jnp.mean(xf * xf, axis=-1, keepdims=True) + EPS)
    return (y * w.astype(jnp.float32)).astype(x.dtype)


def l2_norm(x):
    return x * lax.rsqrt(jnp.sum(x * x, axis=-1, keepdims=True) + EPS)


def rope_tables(seq, dim):
    inv_freq = ROPE_THETA ** (-jnp.arange(0, dim, 2, dtype=jnp.float32) / dim)
    ang = jnp.arange(seq, dtype=jnp.float32)[:, None] * inv_freq[None, :]
    ang = jnp.concatenate([ang, ang], axis=-1)
    return jnp.cos(ang), jnp.sin(ang)


def rotary(x, cos, sin):
    half = x.shape[-1] // 2
    rot = jnp.concatenate([-x[..., half:], x[..., :half]], axis=-1)
    return x * cos[:, None, :] + rot * sin[:, None, :]


def centred_depthwise_conv(x, w):
    pad = (w.shape[0] - 1) // 2
    return lax.conv_general_dilated(
        x, w[:, None, :].astype(x.dtype), window_strides=(1,), padding=[(pad, pad)],
        dimension_numbers=('NWC', 'WIO', 'NWC'), feature_group_count=x.shape[-1])


def gated_delta_rule_chunked(q, k, v, g, beta):
    b, t, h, dk = q.shape
    dv = v.shape[-1]
    c = DN_CHUNK
    n = t // c

    def to_chunks(a):
        return jnp.moveaxis(a.reshape(b, n, c, h, *a.shape[3:]), 3, 1)

    q, k, v, g, beta = (to_chunks(a) for a in (q, k, v, g, beta))
    g = jnp.cumsum(g, axis=-1)
    incl = jnp.tril(jnp.ones((c, c), dtype=bool))
    strict = jnp.tril(jnp.ones((c, c), dtype=bool), -1)
    decay = jnp.exp(jnp.where(incl, g[..., :, None] - g[..., None, :], -jnp.inf))
    k_beta = k * beta[..., None]
    m = jnp.where(strict, jnp.einsum('bhncd,bhnsd->bhncs', k_beta, k) * decay, 0.0)
    eye = jnp.eye(c, dtype=q.dtype)
    t_inv = lax.linalg.triangular_solve(eye + m, jnp.broadcast_to(eye, m.shape),
                                        left_side=True, lower=True, unit_diagonal=True)
    u = jnp.einsum('bhncs,bhnse->bhnce', t_inv, v * beta[..., None])
    w = jnp.einsum('bhncs,bhnsd->bhncd', t_inv, k_beta * jnp.exp(g)[..., None])
    a_intra = jnp.where(incl, jnp.einsum('bhncd,bhnsd->bhncs', q, k) * decay, 0.0)

    def step(state, inp):
        q_i, k_i, u_i, w_i, g_i, a_i = inp
        v_new = u_i - jnp.einsum('bhcd,bhde->bhce', w_i, state)
        o = (jnp.einsum('bhcd,bhde->bhce', q_i * jnp.exp(g_i)[..., None], state)
             + jnp.einsum('bhcs,bhse->bhce', a_i, v_new))
        g_last = g_i[..., -1]
        k_dec = k_i * jnp.exp(g_last[..., None] - g_i)[..., None]
        state = state * jnp.exp(g_last)[..., None, None] + jnp.einsum('bhcd,bhce->bhde', k_dec, v_new)
        return state, o

    xs = tuple(jnp.moveaxis(a, 2, 0) for a in (q, k, u, w, g, a_intra))
    state0 = jnp.zeros((b, h, dk, dv), dtype=q.dtype)
    _, o = lax.scan(step, state0, xs)
    o = jnp.moveaxis(o, 0, 2)
    return jnp.moveaxis(o, 1, 3).reshape(b, t, h, dv)


def gated_deltanet_bidir(q, k, v, beta_logit, decay_logit, conv_w, a_log, dt_bias):
    b, s, _ = q.shape
    qkv = jax.nn.silu(centred_depthwise_conv(jnp.concatenate([q, k, v], axis=-1), conv_w))
    q, k, v = jnp.split(qkv.astype(jnp.float32), 3, axis=-1)
    q = l2_norm(q.reshape(b, s, DN_HEADS, DN_HEAD_DIM)) * (DN_HEAD_DIM ** -0.5)
    k = l2_norm(k.reshape(b, s, DN_HEADS, DN_HEAD_DIM))
    v = v.reshape(b, s, DN_HEADS, DN_HEAD_DIM)
    beta = jax.nn.sigmoid(beta_logit.astype(jnp.float32).reshape(b, s, N_DIR, DN_HEADS))
    dl = decay_logit.astype(jnp.float32).reshape(b, s, N_DIR, DN_HEADS)
    g = -jnp.exp(a_log.astype(jnp.float32)) * jax.nn.softplus(dl + dt_bias.astype(jnp.float32))
    fwd = gated_delta_rule_chunked(q, k, v, g[:, :, 0], beta[:, :, 0])
    flip = lambda a: jnp.flip(a, axis=1)
    bwd = flip(gated_delta_rule_chunked(flip(q), flip(k), flip(v), flip(g[:, :, 1]), flip(beta[:, :, 1])))
    return fwd + bwd


def windowed_gqa_with_sinks(q, k, v, q_norm_w, k_norm_w, sinks):
    b, s, _ = q.shape
    nb = s // BLOCK
    nw = WINDOW // BLOCK
    span = BLOCK + 2 * WINDOW
    grp = SWA_Q_HEADS // SWA_KV_HEADS
    cos, sin = rope_tables(s, SWA_HEAD_DIM)
    q = rotary(rms_norm(q.astype(jnp.float32).reshape(b, s, SWA_Q_HEADS, SWA_HEAD_DIM), q_norm_w), cos, sin)
    k = rotary(rms_norm(k.astype(jnp.float32).reshape(b, s, SWA_KV_HEADS, SWA_HEAD_DIM), k_norm_w), cos, sin)
    v = v.astype(jnp.float32).reshape(b, s, SWA_KV_HEADS, SWA_HEAD_DIM)
    qb = q.reshape(b, nb, BLOCK, SWA_KV_HEADS, grp, SWA_HEAD_DIM)

    def band(a):
        ap = jnp.pad(a, ((0, 0), (WINDOW, WINDOW), (0, 0), (0, 0)))
        ab = ap.reshape(b, nb + 2 * nw, BLOCK, SWA_KV_HEADS, SWA_HEAD_DIM)
        return jnp.concatenate([ab[:, i:i + nb] for i in range(2 * nw + 1)], axis=2)

    kb, vb = band(k), band(v)
    scores = jnp.einsum('bnqhgd,bnkhd->bnhgqk', qb, kb) * (SWA_HEAD_DIM ** -0.5)
    q_pos = jnp.arange(nb)[:, None] * BLOCK + jnp.arange(BLOCK)[None, :]
    k_pos = jnp.arange(nb)[:, None] * BLOCK - WINDOW + jnp.arange(span)[None, :]
    rel = k_pos[:, None, :] - q_pos[:, :, None]
    valid = (jnp.abs(rel) <= WINDOW) & (k_pos[:, None, :] >= 0) & (k_pos[:, None, :] < s)
    scores = jnp.where(valid[None, :, None, None], scores, -jnp.inf)
    sink = sinks.astype(jnp.float32).reshape(SWA_KV_HEADS, grp)[None, None, :, :, None, None]
    mx = jnp.maximum(jnp.max(scores, axis=-1, keepdims=True), sink)
    p = jnp.exp(scores - mx)
    denom = jnp.sum(p, axis=-1, keepdims=True) + jnp.exp(sink - mx)
    out = jnp.einsum('bnhgqk,bnkhd->bnqhgd', p / denom, vb)
    return out.reshape(b, s, SWA_WIDTH)


def setup_inputs(seed: int = 0) -> dict:
    key = jax.random.key(seed)
    ks = jax.random.split(key, 12)
    x = jax.random.normal(ks[0], (BATCH, SEQ, D_MODEL), jnp.float32)
    norm_w = 1.0 + 0.02 * jax.random.normal(ks[1], (DEPTH, D_MODEL), jnp.float32)
    w_in = jax.random.normal(ks[2], (DEPTH, D_MODEL, IN_WIDTH), jnp.float32) * D_MODEL ** -0.5
    dn_conv_w = jax.random.normal(ks[3], (DEPTH, DN_CONV, 3 * DN_WIDTH), jnp.float32) * DN_CONV ** -0.5
    dn_a_log = jnp.log(jax.random.uniform(ks[4], (DEPTH, N_DIR, DN_HEADS), jnp.float32, 1.0, 16.0))
    dt = jnp.exp(jax.random.uniform(ks[5], (DEPTH, N_DIR, DN_HEADS), jnp.float32,
                                    float(np.log(1e-3)), float(np.log(1e-1))))
    dn_dt_bias = dt + jnp.log(-jnp.expm1(-dt))
    dn_out_norm_w = 1.0 + 0.02 * jax.random.normal(ks[6], (DEPTH, DN_HEAD_DIM), jnp.float32)
    swa_q_norm_w = 1.0 + 0.02 * jax.random.normal(ks[7], (DEPTH, SWA_HEAD_DIM), jnp.float32)
    swa_k_norm_w = 1.0 + 0.02 * jax.random.normal(ks[8], (DEPTH, SWA_HEAD_DIM), jnp.float32)
    swa_sinks = jax.random.normal(ks[9], (DEPTH, SWA_Q_HEADS), jnp.float32)
    w_out = jax.random.normal(ks[10], (DEPTH, MIX_WIDTH, D_MODEL), jnp.float32) * MIX_WIDTH ** -0.5
    return {'x': x, 'norm_w': norm_w, 'w_in': w_in, 'dn_conv_w': dn_conv_w, 'dn_a_log': dn_a_log,
            'dn_dt_bias': dn_dt_bias, 'dn_out_norm_w': dn_out_norm_w, 'swa_q_norm_w': swa_q_norm_w,
            'swa_k_norm_w': swa_k_norm_w, 'swa_sinks': swa_sinks, 'w_out': w_out}


def reference(x, norm_w, w_in, dn_conv_w, dn_a_log, dn_dt_bias, dn_out_norm_w,
              swa_q_norm_w, swa_k_norm_w, swa_sinks, w_out):
    b, s, _ = x.shape
    cuts = []
    acc = 0
    for width in SPLITS[:-1]:
        acc += width
        cuts.append(acc)
    for l in range(DEPTH):
        h = rms_norm(x, norm_w[l])
        proj = h @ w_in[l].astype(h.dtype)
        (dn_q, dn_k, dn_v, dn_z, dn_beta, dn_decay,
         sw_q, sw_k, sw_v, sw_z) = jnp.split(proj, cuts, axis=-1)
        dn = gated_deltanet_bidir(dn_q, dn_k, dn_v, dn_beta, dn_decay,
                                  dn_conv_w[l], dn_a_log[l], dn_dt_bias[l])
        dn = rms_norm(dn, dn_out_norm_w[l]).reshape(b, s, DN_WIDTH)
        dn = dn * jax.nn.silu(dn_z.astype(jnp.float32))
        sw = windowed_gqa_with_sinks(sw_q, sw_k, sw_v, swa_q_norm_w[l], swa_k_norm_w[l], swa_sinks[l])
        sw = sw * jax.nn.silu(sw_z.astype(jnp.float32))
        mix = jnp.concatenate([dn, sw], axis=-1).astype(x.dtype)
        x = x + mix @ w_out[l].astype(x.dtype)
    return x
```

```python
import numpy as np
import ml_dtypes
from contextlib import ExitStack
import concourse.bass as bass
import concourse.mybir as mybir
from concourse.bass_utils import run_bass_kernel_spmd

F32 = mybir.dt.float32
BF = mybir.dt.bfloat16
AF = mybir.ActivationFunctionType
OP = mybir.AluOpType

T = 8192
D = 1024
KC = 8
NT = 16
NB = 64
EPS = 1e-6
NLEV = 7


class Em:
    def __init__(self, nc, es):
        self.nc = nc
        self.eng = {'pe': nc.tensor, 'act': nc.scalar, 'dve': nc.vector, 'pool': nc.gpsimd, 'sp': nc.sync}
        self.csem = {e: es.enter_context(nc.semaphore('c_' + e)) for e in ('pe', 'act', 'dve', 'pool')}
        self.ccnt = {e: 0 for e in self.csem}
        self.es = es
        self.dsem = {}
        self.dcnt = {}
        self.res = {}
        self.waited = {e: {} for e in self.eng}
        self.pend = {e: ([], []) for e in self.eng}

    def _wait(self, e, ev):
        if ev[0] == 'c':
            if ev[1] == e and e == 'pe':
                return
            sem, val, k = self.csem[ev[1]], ev[2], ('c', ev[1])
        else:
            sem, val, k = self.dsem[ev[1]], 16 * self.dcnt[ev[1]], ('d', ev[1])
        if self.waited[e].get(k, 0) >= val:
            return
        self.waited[e][k] = val
        self.eng[e].wait_ge(sem, val)

    def deps(self, e, reads, writes):
        for r in reads:
            st = self.res.get(r)
            if st:
                for ev in st['w']:
                    self._wait(e, ev)
        for w in writes:
            st = self.res.get(w)
            if st:
                for ev in st['w']:
                    self._wait(e, ev)
                for ev in st['r'].values():
                    self._wait(e, ev)

    def record(self, ev, reads, writes):
        for r in reads:
            st = self.res.setdefault(r, {'w': [], 'r': {}})
            st['r'][ev[:2]] = ev
        for w in writes:
            self.res[w] = {'w': [ev], 'r': {}}

    def op(self, e, fn, reads, writes, inc=True):
        psr = [k for k in reads if isinstance(k, tuple) and k and k[0] == 'ps']
        if psr:
            reads = [k for k in reads if k not in psr]
            writes = list(writes) + [k for k in psr if k not in writes]
        self.deps(e, reads, writes)
        ins = fn(self.eng[e])
        if inc:
            self.ccnt[e] += 1
            ins.then_inc(self.csem[e], 1)
            pr, pw = self.pend[e]
            self.record(('c', e, self.ccnt[e]), list(reads) + pr, list(writes) + pw)
            self.pend[e] = ([], [])
        else:
            self.pend[e][0].extend(reads)
            self.pend[e][1].extend(writes)
        return ins

    def dma(self, out, in_, key, reads, writes, q='sp'):
        if key not in self.dsem:
            self.dsem[key] = self.es.enter_context(self.nc.semaphore('d_%d' % len(self.dsem)))
            self.dcnt[key] = 0
        self.deps(q, reads, writes)
        ins = self.eng[q].dma_start(out=out, in_=in_)
        self.dcnt[key] += 1
        ins.then_inc(self.dsem[key], 16)
        self.record(('d', key), reads, writes)

    def barrier(self):
        for e in self.eng:
            for k in self.dsem:
                self._wait(e, ('d', k))
            for c in self.csem:
                if self.ccnt[c]:
                    self._wait(e, ('c', c, self.ccnt[c]))


STAGES = {'s0', 'swa', 'gdn', 'out'}
NCORES = 4
GDN_HEADS = 2
TH = T // 2
GDN_LEVEL = 99
SWA_PARTS = {'kv', 'q', 'attn'}


def build_program():
    nc = bass.Bass("TRN2", target_bir_lowering=False, num_devices=NCORES)
    es = ExitStack()
    dram_in = {}

    def din(name, shape, dt=F32):
        dram_in[name] = nc.dram_tensor(name, list(shape), dt, kind="ExternalInput").ap()
        return dram_in[name]

    xT = din("xTh", [D, T // 2])
    xtok = din("xtok", [TH, D])
    normw = din("normw", [128, KC])
    wgdn = din("wgdn", [2, D, 516])
    wswk = din("wswk", [D, 320])
    wswq = din("wswq", [D, 768])
    wout = din("wout", [D, D])
    convw = din("convw", [128, 6, 5])
    gpar = din("gpar", [128, 8])
    onw = din("onw", [128, 1])
    qkw = din("qkw", [128, 4])
    esk = din("sinks", [128, 2])
    cmask = din("cmask", [128, 12, 128])
    cbf = din("cbf", [128, 6, 128], BF)
    rope = din("rope", [128, 2, T])
    y = nc.dram_tensor("y", [TH, D], F32, kind="ExternalOutput").ap()
    xb_d = nc.dram_tensor("xb_sh", [D, T], BF, addr_space="Shared").ap()
    rstd_d = nc.dram_tensor("rstd_sh", [128, T], F32, addr_space="Shared").ap()
    xb_l = nc.dram_tensor("xb_l", [D, T // 2], BF).ap()
    rstd_l = nc.dram_tensor("rstd_l", [128, T // 2], F32).ap()
    mix_d = nc.dram_tensor("mix_sh", [D, T], BF, addr_space="Shared").ap()
    mix_loc = nc.dram_tensor("mix_loc", [512, T], BF).ap()
    mix_in = nc.dram_tensor("mix_in", [D, TH], BF).ap()
    rank = nc.gpsimd.partition_id() % 2

    em = Em(nc, es)

    stk = {'cur': es}

    def sb(name, shape, dt):
        return stk['cur'].enter_context(nc.sbuf_tensor(name, list(shape), dt))

    WS = {}

    ps = [es.enter_context(nc.psum_tensor("ps%d" % i, [128, 512], F32)) for i in range(8)]

    mk = sb("mk", [128, 12, 128], F32)
    cb = sb("cb", [128, 6, 128], BF)
    nw = sb("nw", [128, KC], F32)
    cw = sb("cw", [128, 6, 5], F32)
    gp = sb("gp", [128, 8], F32)
    onw_s = sb("onw_s", [128, 1], F32)
    qkw_s = sb("qkw_s", [128, 4], F32)
    esk_s = sb("esk_s", [128, 2], F32)
    for dst, src, k in ((mk, cmask, 'mk'), (cb, cbf, 'cb'), (nw, normw, 'nw'), (cw, convw, 'cw'), (gp, gpar, 'gp'),
                        (onw_s, onw, 'onw'), (qkw_s, qkw, 'qkw'), (esk_s, esk, 'esk')):
        em.dma(dst[:], src, 'const', [], [k])
    M_CUMF, M_CUMB, M_ASF, M_ASB, M_TIF, M_TIB, M_TSF, M_TSB = range(8)
    ident = cb[:, 0, :]
    ones_bf = cb[:, 1, :]
    blk1 = cb[:, 2, :]
    identf = sb("identf", [128, 128], F32)
    onesf = sb("onesf", [128, 128], F32)
    em.op('dve', lambda e: e.tensor_copy(out=identf[:], in_=ident), ['cb'], ['identf'])
    em.op('dve', lambda e: e.tensor_copy(out=onesf[:], in_=ones_bf), ['cb'], ['onesf'])
    ealog = sb("ealog", [128, 4], F32)
    em.op('act', lambda e: e.activation(out=ealog[:], in_=gp[:, 0:4], func=AF.Exp), ['gp'], ['ealog'])
    em.op('act', lambda e: e.activation(out=esk_s[:], in_=esk_s[:], func=AF.Exp), ['esk'], ['esk'])

    XS = {}

    def alloc_xs(n, tag):
        XS['b'] = [sb("xs%s%d" % (tag, i), [128, KC, 512], F32) for i in range(n)]
    xbt = [sb("xbt%d" % i, [128, KC, 512], BF) for i in range(2)]
    rst = [sb("rst%d" % i, [128, 512], F32) for i in range(2)]
    tA = sb("tA", [128, 512], F32)

    def rsqrt(out_ap, in_ap, scale, rd, wr, tmp, tmpk):
        em.op('act', lambda e: e.activation(out=tmp, in_=in_ap, func=AF.Ln, bias=epsb[:, 0:1], scale=scale), rd + ['epsb'], [tmpk])
        em.op('act', lambda e: e.activation(out=out_ap, in_=tmp, func=AF.Exp, scale=-0.5), [tmpk], wr)

    epsb = sb("epsb", [128, 1], F32)
    em.op('dve', lambda e: e.memset(epsb[:], EPS), [], ['epsb'])

    def load_weights(src_ap, ncols, extra_w=()):
        c0 = 0
        i = 0
        while c0 < ncols:
            w = min(512, ncols - c0)
            nb_ = len(XS['b'])
            s = XS['b'][i % nb_]
            em.dma(s[:, :, 0:w], src_ap.rearrange("(kc p) n -> p kc n", p=128)[:, :, c0:c0 + w], ('xs', i % nb_), [], [('xs', i % nb_)] + list(extra_w))
            for kc in range(KC):
                em.op('dve', lambda e, kc=kc, s=s, c0=c0, w=w: e.tensor_scalar(out=WS['w'][:, kc, c0:c0 + w], in0=s[:, kc, 0:w], scalar1=nw[:, kc:kc + 1], scalar2=None, op0=OP.mult),
                      [('xs', i % nb_), 'nw'], ['wsb'])
            c0 += w
            i += 1

    stk['cur'] = ExitStack()
    sq = sb("sq", [128, KC, 512], BF)
    alloc_xs(2, 'p')
    xs = XS['b']
    xTv = xT.rearrange("(kc p) n -> p kc n", p=128)
    xbv = xb_d.rearrange("(kc p) n -> p kc n", p=128)
    xblv = xb_l.rearrange("(kc p) n -> p kc n", p=128)
    for t in range(NT // 2):
        sl = t % 2
        tok = slice(t * 512, (t + 1) * 512)
        em.dma(xs[sl][:], xTv[:, :, tok], ('xs', sl), [], [('xs', sl)])
        em.op('act', lambda e: e.activation(out=sq[:], in_=xs[sl][:], func=AF.Square), [('xs', sl)], ['sq'])
        em.op('dve', lambda e: e.tensor_copy(out=xbt[sl][:], in_=xs[sl][:]), [('xs', sl)], [('xbt', sl)])
        for kc in range(KC):
            em.op('pe', lambda e, kc=kc: e.matmul(ps[0][:, :], lhsT=ones_bf, rhs=sq[:, kc, :], start=(kc == 0), stop=(kc == KC - 1)),
                  ['sq', 'cb'], [('ps', 0)], inc=(kc == KC - 1))
        rsqrt(rst[sl][:], ps[0][:, :], 1.0 / D, [('ps', 0)], [('rst', sl)], tA[:], 'tA')
        em.dma(xblv[:, :, tok], xbt[sl][:], ('xbt', sl), [('xbt', sl)], [])
        em.dma(rstd_l[:, tok], rst[sl][:], ('rst', sl), [('rst', sl)], [])
    em.barrier()
    em.dma(xb_d[:, bass.ds(rank * (T // 2), T // 2)], xb_l, 'xch0', [], [], q='pool')
    em.dma(rstd_d[:, bass.ds(rank * (T // 2), T // 2)], rstd_l, 'xch0', [], [], q='pool')
    em.barrier()
    nc.all_core_barrier()
    stk['cur'].close()
    stk['cur'] = es

    def load_tile(t):
        sl = t % 2
        tok = slice(t * 512, (t + 1) * 512)
        em.dma(xbt[sl][:], xbv[:, :, tok], ('xbt', sl), [], [('xbt', sl)])
        em.dma(rst[sl][:], rstd_d[:, tok], ('rst', sl), [], [('rst', sl)])
        return sl

    def project(sl, c0, m, bank):
        for kc in range(KC):
            em.op('pe', lambda e, kc=kc: e.matmul(ps[bank][0:m, :], lhsT=WS['w'][:, kc, c0:c0 + m], rhs=xbt[sl][:, kc, :], start=(kc == 0), stop=(kc == KC - 1)),
                  ['wsb', ('xbt', sl)], [('ps', bank)], inc=(kc == KC - 1))

    SWA_ON = 'swa' in STAGES
    stk['cur'] = ExitStack()
    WS['w'] = sb("wsb_swa", [128, KC, 768], BF)
    alloc_xs(2, 's')
    kAB = sb("kAB", [128, 1, T], BF)
    vpad = sb("vpad", [128, NB, 1, 192], BF)
    opad = sb("opad", [128, 192], BF)
    em.op('dve', lambda e: e.memset(vpad[:], 0.0), [], ['vpad'])
    em.op('dve', lambda e: e.memset(opad[:], 0.0), [], ['opad'])
    em.op('dve', lambda e: e.memset(opad[:, 64:128], 1.0), ['opad'], ['opad'])
    rp = [sb("rp%d" % i, [128, 2, 512], F32) for i in range(2)]
    smask = sb("smask", [128, 384], BF)
    em.op('dve', lambda e: e.tensor_copy(out=smask[:, 0:128], in_=cb[:, 3, :]), ['cb'], ['smask'])
    em.op('dve', lambda e: e.tensor_copy(out=smask[:, 128:256], in_=cb[:, 1, :]), ['cb', 'smask'], ['smask'])
    em.op('dve', lambda e: e.tensor_copy(out=smask[:, 256:384], in_=cb[:, 4, :]), ['cb', 'smask'], ['smask'])
    NSET = 2
    uA = [sb("uA%d" % s, [128, 512], F32) for s in range(NSET)]
    uB = [sb("uB%d" % s, [128, 512], F32) for s in range(NSET)]
    uC = [sb("uC%d" % s, [128, 512], F32) for s in range(NSET)]
    uD = [sb("uD%d" % s, [128, 512], BF) for s in range(NSET)]

    def run_interleaved(gens):
        gens = [g for g in gens if g is not None]
        while gens:
            for g in list(gens):
                try:
                    next(g)
                except StopIteration:
                    gens.remove(g)

    def qk_chain(sl, s, c0, c0rot, wcol, out_ap, outk, bA, bB):
        A, B_, C, Dq = uA[s], uB[s], uC[s], uD[s]
        kA, kB_, kC, kD = ('uA', s), ('uB', s), ('uC', s), ('uD', s)
        project(sl, c0, 128, bA)
        project(sl, c0rot, 128, bB)
        yield
        em.op('dve', lambda e: e.tensor_tensor(out=A[:], in0=ps[bA][:, :], in1=rst[sl][:], op=OP.mult), [('ps', bA), ('rst', sl)], [kA])
        em.op('act', lambda e: e.activation(out=Dq[:], in_=A[:], func=AF.Square), [kA], [kD])
        em.op('dve', lambda e: e.tensor_tensor(out=B_[:], in0=ps[bB][:, :], in1=rst[sl][:], op=OP.mult), [('ps', bB), ('rst', sl)], [kB_])
        yield
        em.op('pe', lambda e: e.matmul(ps[bA][:, :], lhsT=blk1, rhs=Dq[:], start=True, stop=True), [kD, 'cb'], [('ps', bA)])
        yield
        em.op('act', lambda e: e.activation(out=C[:], in_=ps[bA][:, :], func=AF.Ln, bias=epsb[:, 0:1], scale=1.0 / 64), [('ps', bA), 'epsb'], [kC])
        yield
        em.op('act', lambda e: e.activation(out=C[:], in_=C[:], func=AF.Exp, scale=-0.5), [kC], [kC])
        em.op('dve', lambda e: e.scalar_tensor_tensor(out=A[:], in0=A[:], scalar=qkw_s[:, wcol:wcol + 1], in1=C[:], op0=OP.mult, op1=OP.mult), [kA, kC, 'qkw'], [kA])
        em.op('dve', lambda e: e.scalar_tensor_tensor(out=B_[:], in0=B_[:], scalar=qkw_s[:, wcol + 1:wcol + 2], in1=C[:], op0=OP.mult, op1=OP.mult), [kB_, kC, 'qkw'], [kB_])
        yield
        em.op('pool', lambda e: e.tensor_tensor(out=A[:], in0=A[:], in1=rp[sl][:, 0, :], op=OP.mult), [kA, ('rp', sl)], [kA])
        em.op('dve', lambda e: e.tensor_tensor(out=B_[:], in0=B_[:], in1=rp[sl][:, 1, :], op=OP.mult), [kB_, ('rp', sl)], [kB_])
        yield
        em.op('dve', lambda e: e.tensor_tensor(out=out_ap, in0=A[:], in1=B_[:], op=OP.add), [kA, kB_], [outk])
        yield

    def load_tile_rope(t):
        sl = load_tile(t)
        em.dma(rp[sl][:], rope[:, :, t * 512:(t + 1) * 512], ('rp', sl), [], [('rp', sl)])
        return sl

    load_weights(wswk, 320)

    def kv_tile(t):
        s = t % 2
        sl = load_tile_rope(t)
        tok = slice(t * 512, (t + 1) * 512)
        yield from qk_chain(sl, s, 0, 128, 2, kAB[:, 0, tok], 'kAB', 2 * s, 2 * s + 1)
        bV, bT = 4 + 2 * s, 5 + 2 * s
        project(sl, 256, 64, bV)
        yield
        em.op('dve', lambda e: e.tensor_tensor(out=uD[s][0:64, :], in0=ps[bV][0:64, :], in1=rst[sl][0:64, :], op=OP.mult), [('ps', bV), ('rst', sl)], [('uD', s)])
        yield
        for j in range(4):
            em.op('pe', lambda e, j=j: e.matmul(ps[bT][:, j * 64:(j + 1) * 64], lhsT=uD[s][0:64, j * 128:(j + 1) * 128], rhs=cb[0:64, 0, 0:64], start=True, stop=True),
                  [('uD', s), 'cb'], [('ps', bT)], inc=(j == 3))
        yield
        em.op('act', lambda e: e.activation(out=vpad[:, t * 4:(t + 1) * 4, 0, 64:128], in_=ps[bT][:, 0:256].rearrange("p (j c) -> p j c", j=4), func=AF.Copy),
              [('ps', bT)], ['vpad'])
        yield

    for t in range(0, NT if (SWA_ON and 'kv' in SWA_PARTS) else 0, 2):
        run_interleaved([kv_tile(t), kv_tile(t + 1)])

    load_weights(wswq, 768)
    qr = [sb("qr%d" % i, [128, 2, 512], BF) for i in range(2)]
    zs4 = [sb("zs4%d" % i, [128, 2, 512], BF) for i in range(2)]
    pT = [[sb("pT%d_%d" % (g, i), [128, 384], BF) for i in range(2)] for g in range(2)]
    mixo = [sb("mixo%d" % i, [128, 512], BF) for i in range(2)]
    fA = sb("fA", [128, 512], F32)

    def q_tile(t):
        qb = t % 2
        sl = load_tile_rope(t)
        for c in range(2):
            s = c % NSET
            bA = c % 2
            project(sl, 512 + c * 128, 128, bA)
            yield
            em.op('dve', lambda e: e.tensor_tensor(out=uA[s][:], in0=ps[bA][:, :], in1=rst[sl][:], op=OP.mult), [('ps', bA), ('rst', sl)], [('uA', s)])
            yield
            em.op('act', lambda e, c=c: e.activation(out=zs4[qb][:, c, :], in_=uA[s][:], func=AF.Silu), [('uA', s)], [('zs4', qb)])
            yield
        for c in range(2):
            yield from qk_chain(sl, c % NSET, c * 128, 256 + c * 128, 0, qr[qb][:, c, :], ('qr', qb), 0, 1)

    def attn_tile(t):
        qb = t % 2
        tok = slice(t * 512, (t + 1) * 512)
        groups = [(c, q4) for c in range(2) for q4 in range(4)]

        def scores(gi):
            c, q4 = groups[gi]
            i = t * 4 + q4
            js = [j for j in (i - 1, i, i + 1) if 0 <= j < NB]
            gp_ = gi % 2
            for hh in range(2):
                prt = slice(hh * 64, hh * 64 + 64)
                bank = 2 + 2 * gp_ + hh
                pt = pT[gp_][hh]
                for j in js:
                    so = (j - i + 1) * 128
                    em.op('pe', lambda e, j=j, so=so: e.matmul(ps[bank][:, so:so + 128], lhsT=kAB[prt, 0, j * 128:(j + 1) * 128],
                                                             rhs=qr[qb][prt, c, q4 * 128:(q4 + 1) * 128], start=True, stop=True),
                          ['kAB', ('qr', qb)], [('ps', bank)], inc=(j == js[-1]))
                lo, hi = (js[0] - i + 1) * 128, (js[-1] - i + 2) * 128
                em.op('act', lambda e: e.activation(out=pt[:, lo:hi], in_=ps[bank][:, lo:hi], func=AF.Exp, scale=0.125), [('ps', bank)], [('pT', gp_, hh)])
                meng = 'dve' if hh == 0 else 'pool'
                em.op(meng, lambda e: e.tensor_tensor(out=pt[:, lo:hi], in0=pt[:, lo:hi], in1=smask[:, lo:hi], op=OP.mult), [('pT', gp_, hh), 'smask'], [('pT', gp_, hh)])

        def pv(gi):
            c, q4 = groups[gi]
            i = t * 4 + q4
            js = [j for j in (i - 1, i, i + 1) if 0 <= j < NB]
            gp_ = gi % 2
            for which in range(2):
                bank = 6 + which
                n = 0
                for hh in range(2):
                    for j in js:
                        so = (j - i + 1) * 128
                        win = slice(64, 192) if hh == 0 else slice(0, 128)
                        lhs = vpad[:, j, 0, win] if which == 0 else opad[:, win]
                        n += 1
                        em.op('pe', lambda e, lhs=lhs, so=so, hh=hh, n=n: e.matmul(ps[bank][:, q4 * 128:(q4 + 1) * 128], lhsT=lhs, rhs=pT[gp_][hh][:, so:so + 128],
                                                                                 start=(n == 1), stop=(n == 2 * len(js))),
                              ['vpad', 'opad', ('pT', gp_, 0), ('pT', gp_, 1)], [('ps', bank)], inc=(n == 2 * len(js)))
            if q4 == 3:
                msl = c % 2
                em.op('act', lambda e: e.activation(out=fA[:], in_=ps[7][:, :], func=AF.Ln, bias=esk_s[:, c:c + 1], scale=1.0), [('ps', 7), 'esk'], ['fA'])
                em.op('act', lambda e: e.activation(out=fA[:], in_=fA[:], func=AF.Exp, scale=-1.0), ['fA'], ['fA'])
                em.op('dve', lambda e: e.tensor_tensor(out=fA[:], in0=ps[6][:, :], in1=fA[:], op=OP.mult), [('ps', 6), 'fA'], ['fA'])
                em.op('dve', lambda e: e.tensor_tensor(out=mixo[msl][:], in0=fA[:], in1=zs4[qb][:, c, :], op=OP.mult), ['fA', ('zs4', qb)], [('mixo', msl)])
                em.dma(mix_loc[256 + c * 128:256 + (c + 1) * 128, tok], mixo[msl][:], ('mixo', msl), [('mixo', msl)], [])

        for gi in range(len(groups) + 1):
            if gi < len(groups):
                scores(gi)
                yield
            if gi >= 1:
                pv(gi - 1)
                yield

    for t in range(NT + 1 if SWA_ON else 0):
        run_interleaved([attn_tile(t - 1) if (t > 0 and 'attn' in SWA_PARTS) else None, q_tile(t) if (t < NT and 'q' in SWA_PARTS) else None])
    em.barrier()
    stk['cur'].close()
    stk['cur'] = ExitStack()
    WS['w'] = sb("wsb_gdn", [128, KC, 528], BF)

    qk = sb("qk", [128, NB, 2, 128], BF)
    ktok = sb("ktok", [128, NB, 128], BF)
    vtok = sb("vtok", [128, NB, 128], BF)
    zs = sb("zs", [128, T], BF)
    oacc = sb("oacc", [128, NB, 128], F32)
    XS['b'] = [oacc[:, 0:32, :].rearrange("p (a b) c -> p a (b c)", a=8, b=4)]
    pre = [sb("pre%d" % i, [128, 3, 516], BF) for i in range(3)]
    wdiag = sb("wdiag", [128, 15, 128], BF)
    bdT = sb("bdT", [4, 512], F32)
    graw = sb("graw", [128, NB, 4], F32)
    gt = sb("gt", [128, 14, NB], F32)
    egl = sb("egl", [128, 2, NB], F32)
    Bm = [[sb("Bm%d_%d" % (s, d), [128, 128], F32) for d in range(2)] for s in range(2)]
    EE = [[sb("EE%d_%d" % (s, d), [128, 256], F32) for d in range(2)] for s in range(2)]
    E1 = [[sb("E1%d_%d" % (s, d), [128, 128], F32) for d in range(2)] for s in range(2)]
    Wb = [[sb("Wb%d_%d" % (d, p), [128, 512], BF) for p in range(4)] for d in range(2)]
    osq = sb("osq", [128, 128], F32)
    cA = [sb("cA%d" % s, [128, 512], F32) for s in range(2)]
    cC = [sb("cC%d" % s, [128, 512], F32) for s in range(2)]
    cD = [sb("cD%d" % s, [128, 512], BF) for s in range(2)]
    pz = sb("pz", [128, 512], F32)
    on2 = sb("on2", [128, 128], F32)
    Sf = [sb("Sf%d" % d, [128, 128], F32) for d in range(2)]
    Sb = [sb("Sb%d" % d, [128, 128], BF) for d in range(2)]
    rr = [sb("rr%d" % d, [128, 128], BF) for d in range(2)]
    vn = [sb("vn%d" % d, [128, 128], BF) for d in range(2)]
    kd = [sb("kd%d" % d, [128, 128], BF) for d in range(2)]
    ot = [sb("ot%d" % d, [128, 128], F32) for d in range(2)]
    on = sb("on", [128, 128], BF)
    ssq = sb("ssq", [128, 2], F32)
    mixg = [sb("mixg%d" % i, [128, 128], BF) for i in range(2)]

    for h in range(GDN_HEADS if 'gdn' in STAGES else 0):
        load_weights(wgdn[h], 516, extra_w=[('oacc', n) for n in range(NB)])
        for c3 in range(3):
            for tap in range(5):
                em.op('dve', lambda e, c3=c3, tap=tap: e.tensor_scalar(out=wdiag[:, c3 * 5 + tap, :], in0=identf[:], scalar1=cw[:, c3 * 2 + h, tap:tap + 1], scalar2=None, op0=OP.mult),
                      ['identf', 'cw'], ['wdiag'])
        for s3 in range(3):
            em.op('dve', lambda e, s3=s3: e.memset(pre[s3][:], 0.0), [], [('pre', s3)])

        def conv_tile(tt):
            s3 = tt % 3
            for c3 in range(3):
                s = c3 % 2
                cb_ = 4 + s
                A, C, Dq = cA[s], cC[s], cD[s]
                kA, kC, kD = ('cA', s), ('cC', s), ('cD', s)
                for tap in range(5):
                    em.op('pe', lambda e, c3=c3, tap=tap: e.matmul(ps[cb_][:, :], lhsT=wdiag[:, c3 * 5 + tap, :], rhs=pre[s3][:, c3, tap:tap + 512], start=(tap == 0), stop=(tap == 4)),
                          ['wdiag', ('pre', s3)], [('ps', cb_)], inc=(tap == 4))
                yield
                em.op('act', lambda e: e.activation(out=A[:], in_=ps[cb_][:, :], func=AF.Silu), [('ps', cb_)], [kA])
                if c3 < 2:
                    em.op('act', lambda e: e.activation(out=Dq[:], in_=A[:], func=AF.Square), [kA], [kD])
                    yield
                    em.op('pe', lambda e: e.matmul(ps[cb_][:, :], lhsT=ones_bf, rhs=Dq[:], start=True, stop=True), [kD, 'cb'], [('ps', cb_)])
                    yield
                    em.op('act', lambda e: e.activation(out=C[:], in_=ps[cb_][:, :], func=AF.Ln, bias=epsb[:, 0:1], scale=1.0), [('ps', cb_), 'epsb'], [kC])
                    em.op('act', lambda e: e.activation(out=C[:], in_=C[:], func=AF.Exp, scale=-0.5), [kC], [kC])
                    scl = (128.0 ** -0.5) if c3 == 0 else 1.0
                    em.op('dve', lambda e, c3=c3, scl=scl: e.scalar_tensor_tensor(out=qk[:, tt * 4:(tt + 1) * 4, c3, :], in0=A[:].rearrange("p (j c) -> p j c", j=4), scalar=scl,
                                                                                in1=C[:].rearrange("p (j c) -> p j c", j=4), op0=OP.mult, op1=OP.mult), [kA, kC], ['qk'])
                    yield
                    if c3 == 1:
                        for j in range(4):
                            em.op('pe', lambda e, j=j: e.matmul(ps[6][:, j * 128:(j + 1) * 128], lhsT=qk[:, tt * 4 + j, 1, :], rhs=ident, start=True, stop=True),
                                  ['qk', 'cb'], [('ps', 6)], inc=(j == 3))
                        yield
                        em.op('act', lambda e: e.activation(out=ktok[:, tt * 4:(tt + 1) * 4, :], in_=ps[6][:, :].rearrange("p (j c) -> p j c", j=4), func=AF.Copy), [('ps', 6)], ['ktok'])
                else:
                    em.op('dve', lambda e: e.tensor_copy(out=Dq[:], in_=A[:]), [kA], [kD])
                    yield
                    for j in range(4):
                        em.op('pe', lambda e, j=j: e.matmul(ps[6][:, j * 128:(j + 1) * 128], lhsT=Dq[:, j * 128:(j + 1) * 128], rhs=ident, start=True, stop=True),
                              [kD, 'cb'], [('ps', 6)], inc=(j == 3))
                    yield
                    em.op('act', lambda e: e.activation(out=vtok[:, tt * 4:(tt + 1) * 4, :], in_=ps[6][:, :].rearrange("p (j c) -> p j c", j=4), func=AF.Copy), [('ps', 6)], ['vtok'])
                yield

        def proj_tile(t):
            sl = load_tile(t)
            s3 = t % 3
            if t >= 2:
                em.op('dve', lambda e: e.memset(pre[s3][:, :, 514:516], 0.0), [('pre', s3)], [('pre', s3)])
            for c3 in range(3):
                project(sl, c3 * 128, 128, c3)
                yield
                em.op('dve', lambda e, c3=c3: e.tensor_tensor(out=pre[s3][:, c3, 2:514], in0=ps[c3][:, :], in1=rst[sl][:], op=OP.mult), [('ps', c3), ('rst', sl)], [('pre', s3)])
            if t > 0:
                sp_ = (t - 1) % 3
                em.op('act', lambda e: e.activation(out=pre[sp_][:, :, 514:516], in_=pre[s3][:, :, 2:4], func=AF.Copy), [('pre', s3), ('pre', sp_)], [('pre', sp_)])
                em.op('act', lambda e: e.activation(out=pre[s3][:, :, 0:2], in_=pre[sp_][:, :, 512:514], func=AF.Copy), [('pre', s3), ('pre', sp_)], [('pre', s3)])
            project(sl, 384, 128, 3)
            yield
            em.op('dve', lambda e: e.tensor_tensor(out=pz[:], in0=ps[3][:, :], in1=rst[sl][:], op=OP.mult), [('ps', 3), ('rst', sl)], ['pz'])
            em.op('act', lambda e: e.activation(out=zs[:, t * 512:(t + 1) * 512], in_=pz[:], func=AF.Silu), ['pz'], ['zs'])
            project(sl, 512, 4, 7)
            yield
            em.op('dve', lambda e: e.tensor_tensor(out=bdT[:], in0=ps[7][0:4, :], in1=rst[sl][0:4, :], op=OP.mult), [('ps', 7), ('rst', sl)], ['bdT'])
            yield
            for j in range(4):
                em.op('pe', lambda e, j=j: e.matmul(ps[7][:, j * 4:(j + 1) * 4], lhsT=bdT[0:4, j * 128:(j + 1) * 128], rhs=identf[0:4, 0:4], start=True, stop=True),
                      ['bdT', 'identf'], [('ps', 7)], inc=(j == 3))
            yield
            em.op('act', lambda e: e.activation(out=graw[:, t * 4:(t + 1) * 4, :], in_=ps[7][:, 0:16].rearrange("p (j c) -> p j c", j=4), func=AF.Copy), [('ps', 7)], ['graw'])
            yield

        for t in range(NT + 2):
            run_interleaved([proj_tile(t) if t < NT else None, conv_tile(t - 2) if (t >= 2 and GDN_LEVEL >= 2) else None])

        for d in range(2 if GDN_LEVEL >= 3 else 0):
            em.op('act', lambda e: e.activation(out=gt[:, d, :], in_=graw[:, :, d], func=AF.Exp, scale=-1.0), ['graw'], ['gt'])
            em.op('dve', lambda e: e.tensor_scalar(out=gt[:, d, :], in0=gt[:, d, :], scalar1=1.0, scalar2=None, op0=OP.add), ['gt'], ['gt'])
            em.op('dve', lambda e: e.reciprocal(out=gt[:, d, :], in_=gt[:, d, :]), ['gt'], ['gt'])
            em.op('dve', lambda e: e.tensor_scalar(out=gt[:, 2 + d, :], in0=gt[:, d, :], scalar1=-1.0, scalar2=None, op0=OP.mult), ['gt'], ['gt'])
            em.op('act', lambda e: e.activation(out=gt[:, 4 + d, :], in_=graw[:, :, 2 + d], func=AF.Exp, bias=gp[:, 4 + d * 2 + h:5 + d * 2 + h], scale=1.0), ['graw', 'gp'], ['gt'])
            em.op('act', lambda e: e.activation(out=gt[:, 4 + d, :], in_=gt[:, 4 + d, :], func=AF.Ln, bias=1.0, scale=1.0), ['gt'], ['gt'])
            em.op('dve', lambda e: e.tensor_scalar(out=gt[:, 4 + d, :], in0=gt[:, 4 + d, :], scalar1=ealog[:, d * 2 + h:d * 2 + h + 1], scalar2=-1.0, op0=OP.mult, op1=OP.mult), ['gt', 'ealog'], ['gt'])
            em.op('pe', lambda e: e.matmul(ps[0][:, 0:NB], lhsT=mk[:, M_CUMF + d, :], rhs=gt[:, 4 + d, :], start=True, stop=True), ['gt', 'mk'], [('ps', 0)])
            em.op('pe', lambda e: e.matmul(ps[1][:, 0:NB], lhsT=onesf[:], rhs=gt[:, 4 + d, :], start=True, stop=True), ['gt', 'onesf'], [('ps', 1)])
            em.op('act', lambda e: e.activation(out=gt[:, 6 + d, :], in_=ps[0][:, 0:NB], func=AF.Exp), [('ps', 0)], ['gt'])
            em.op('dve', lambda e: e.tensor_scalar(out=gt[:, 8 + d, :], in0=gt[:, 6 + d, :], scalar1=-1.0, scalar2=None, op0=OP.mult), ['gt'], ['gt'])
            em.op('act', lambda e: e.activation(out=egl[:, d, :], in_=ps[1][:, 0:NB], func=AF.Exp), [('ps', 1)], ['egl'])
            em.op('act', lambda e: e.activation(out=tA[:, 0:NB], in_=ps[0][:, 0:NB], func=AF.Copy), [('ps', 0)], ['src'])
            em.op('dve', lambda e: e.tensor_tensor(out=tA[:, 0:NB], in0=ps[1][:, 0:NB], in1=tA[:, 0:NB], op=OP.subtract), [('ps', 1), 'src'], ['src'])
            em.op('act', lambda e: e.activation(out=gt[:, 10 + d, :], in_=tA[:, 0:NB], func=AF.Exp), ['src'], ['gt'])
            em.op('dve', lambda e: e.tensor_tensor(out=gt[:, 12 + d, :], in0=gt[:, 10 + d, :], in1=gt[:, d, :], op=OP.mult), ['gt'], ['gt'])
            em.op('dve', lambda e: e.memset(Sf[d][:], 0.0), [], [('S', d)])
            em.op('dve', lambda e: e.memset(Sb[d][:], 0.0), [], [('Sb', d)])

        def pre_step(i):
            sl2 = i % 2
            p4 = i % 4
            blks = (i, NB - 1 - i)
            BK = (sl2 * 2, sl2 * 2 + 1)
            M_NEGI, M_NEGS = 8, 10
            kB = lambda d: ('Bm', sl2, d)
            kE = lambda d: ('EE', sl2, d)
            kW = lambda d: ('Wb', d, p4)
            MM = dict(skip_group_check=True)
            for d in range(2):
                n = blks[d]
                bk = ps[BK[d]]
                em.op('dve', lambda e, d=d, n=n: e.tensor_scalar(out=Bm[sl2][d][:], in0=mk[:, M_CUMF + d, :], scalar1=gt[:, 4 + d, n:n + 1], scalar2=None, op0=OP.mult), ['mk', 'gt'], [kB(d)])
                em.op('pe', lambda e, d=d, bk=bk: e.matmul(bk[:, 0:128], lhsT=mk[:, M_ASF + d, :], rhs=Bm[sl2][d][:], start=True, stop=True, **MM), ['mk', kB(d)], [('ps', BK[d])], inc=False)
                em.op('pe', lambda e, d=d, n=n, bk=bk: e.matmul(bk[:, 256:512], lhsT=qk[:, n, 1, :], rhs=qk[:, n, :, :], start=False, stop=True, **MM), ['qk'], [('ps', BK[d])])
            yield
            if GDN_LEVEL < 5:
                return
            for d in range(2):
                em.op('act', lambda e, d=d: e.activation(out=E1[sl2][d][:], in_=ps[BK[d]][:, 0:128], func=AF.Exp), [('ps', BK[d])], [('E1', sl2, d)])
                em.op('pool', lambda e, d=d: e.tensor_tensor(out=EE[sl2][d][:, 0:128], in0=E1[sl2][d][:], in1=mk[:, M_TIF + d, :], op=OP.mult), [('E1', sl2, d), 'mk'], [kE(d)])
                em.op('pool', lambda e, d=d: e.tensor_tensor(out=EE[sl2][d][:, 128:256], in0=E1[sl2][d][:], in1=mk[:, M_TSF + d, :], op=OP.mult), [('E1', sl2, d), 'mk', kE(d)], [kE(d)])
            yield
            if GDN_LEVEL < 6:
                return
            for d in range(2):
                n = blks[d]
                W = Wb[d][p4]
                em.op('dve', lambda e, d=d, n=n, W=W: e.scalar_tensor_tensor(out=W[:, 0:256], in0=ps[BK[d]][:, 256:512], scalar=gt[:, d, n:n + 1], in1=EE[sl2][d][:], op0=OP.mult, op1=OP.mult),
                      [('ps', BK[d]), kE(d), 'gt'], [kW(d)])
                em.op('pe', lambda e, d=d, W=W: e.matmul(ps[BK[d]][:, 256:384], lhsT=W[:, 128:256], rhs=ident, start=True, stop=True, **MM), [kW(d), 'cb'], [('ps', BK[d])])
                em.op('pool', lambda e, d=d, W=W: e.tensor_tensor(out=W[:, 256:384], in0=ident, in1=W[:, 128:256], op=OP.subtract), [kW(d), 'cb'], [kW(d)])
            yield
            for d in range(2):
                W = Wb[d][p4]
                em.op('act', lambda e, d=d, W=W: e.activation(out=W[:, 384:512], in_=ps[BK[d]][:, 256:384], func=AF.Copy), [('ps', BK[d])], [kW(d)])
            yield
            if GDN_LEVEL < 7:
                return
            for m in range(NLEV):
                lastm = (m == NLEV - 1)
                for d in range(2):
                    W = Wb[d][p4]
                    bk = ps[BK[d]]
                    P, R, PT = W[:, 128:256], W[:, 256:384], W[:, 384:512]
                    if m == 0:
                        em.op('pe', lambda e, bk=bk, P=P, PT=PT: e.matmul(bk[:, 0:128], lhsT=PT, rhs=P, start=True, stop=True, **MM), [kW(d)], [('ps', BK[d])], inc=False)
                        em.op('pe', lambda e, bk=bk, R=R: e.matmul(bk[:, 128:256], lhsT=ident, rhs=R, start=False, stop=True, **MM), [kW(d), 'cb'], [('ps', BK[d])], inc=False)
                    elif not lastm:
                        em.op('pe', lambda e, bk=bk, W=W, PT=PT: e.matmul(bk[:, 0:256], lhsT=PT, rhs=W[:, 128:384], start=True, stop=False, **MM), [kW(d)], [('ps', BK[d])], inc=False)
                        em.op('pe', lambda e, bk=bk, R=R: e.matmul(bk[:, 128:256], lhsT=ident, rhs=R, start=False, stop=True, **MM), [kW(d), 'cb'], [('ps', BK[d])], inc=False)
                    else:
                        em.op('pe', lambda e, bk=bk, R=R, PT=PT: e.matmul(bk[:, 128:256], lhsT=PT, rhs=R, start=True, stop=False, **MM), [kW(d)], [('ps', BK[d])], inc=False)
                        em.op('pe', lambda e, bk=bk, R=R: e.matmul(bk[:, 128:256], lhsT=ident, rhs=R, start=False, stop=True, **MM), [kW(d), 'cb'], [('ps', BK[d])])
                    if not lastm:
                        em.op('pe', lambda e, bk=bk, P=P, PT=PT: e.matmul(bk[:, 256:384], lhsT=P, rhs=PT, start=False, stop=True, **MM), [kW(d)], [('ps', BK[d])])
                yield
                for d in range(2):
                    W = Wb[d][p4]
                    eng = 'act' if (d == 0 or m % 2 == 1) else 'dve'
                    lo, hi = (128, 256) if lastm else (0, 384)
                    if eng == 'act':
                        em.op('act', lambda e, d=d, W=W: e.activation(out=W[:, 128 + lo:128 + hi], in_=ps[BK[d]][:, lo:hi], func=AF.Copy), [('ps', BK[d])], [kW(d)])
                    else:
                        em.op('dve', lambda e, d=d, W=W: e.tensor_copy(out=W[:, 128 + lo:128 + hi], in_=ps[BK[d]][:, lo:hi]), [('ps', BK[d])], [kW(d)])
                yield

        def chain_step(i):
            p3 = i % 4
            kW = lambda d: ('Wb', d, p3)
            blks = (i, NB - 1 - i)
            C_, D_ = (4, 5), (6, 7)
            for d in range(2):
                n = blks[d]
                em.op('pe', lambda e, d=d, n=n: e.matmul(ps[C_[d]][:, 0:128], lhsT=qk[:, n, 1, :], rhs=Sb[d][:], start=True, stop=True), ['qk', ('Sb', d)], [('ps', C_[d])], inc=False)
                em.op('pe', lambda e, d=d, n=n: e.matmul(ps[C_[d]][:, 128:256], lhsT=qk[:, n, 0, :], rhs=Sb[d][:], start=True, stop=True), ['qk', ('Sb', d)], [('ps', C_[d])])
                em.op('dve', lambda e, d=d, n=n: e.tensor_scalar(out=kd[d][:], in0=ktok[:, n, :], scalar1=gt[:, 12 + d, n:n + 1], scalar2=None, op0=OP.mult), ['ktok', 'gt'], [('kd', d)])
            yield
            for d in range(2):
                n = blks[d]
                em.op('dve', lambda e, d=d, n=n: e.scalar_tensor_tensor(out=rr[d][:], in0=ps[C_[d]][:, 0:128], scalar=gt[:, 8 + d, n:n + 1], in1=vtok[:, n, :], op0=OP.mult, op1=OP.add),
                      [('ps', C_[d]), 'gt', 'vtok'], [('rr', d)])
            yield
            for d in range(2):
                n = blks[d]
                em.op('pe', lambda e, d=d: e.matmul(ps[C_[d]][:, 256:384], lhsT=Wb[d][p3][:, 256:384], rhs=rr[d][:], start=True, stop=True), [kW(d), ('rr', d)], [('ps', C_[d])])
                em.op('dve', lambda e, d=d, n=n: e.tensor_scalar(out=ot[d][:], in0=ps[C_[d]][:, 128:256], scalar1=gt[:, 6 + d, n:n + 1], scalar2=None, op0=OP.mult), [('ps', C_[d]), 'gt'], [('ot', d)])
            yield
            if GDN_LEVEL < 9:
                return
            for d in range(2):
                n = blks[d]
                em.op('act', lambda e, d=d: e.activation(out=vn[d][:], in_=ps[C_[d]][:, 256:384], func=AF.Copy), [('ps', C_[d])], [('vn', d)])
            yield
            for d in range(2):
                em.op('pe', lambda e, d=d: e.matmul(ps[D_[d]][:, 0:128], lhsT=kd[d][:], rhs=vn[d][:], start=True, stop=True), [('kd', d), ('vn', d)], [('ps', D_[d])], inc=False)
                em.op('pe', lambda e, d=d: e.matmul(ps[D_[d]][:, 128:256], lhsT=Wb[d][p3][:, 0:128], rhs=vn[d][:], start=True, stop=True), [kW(d), ('vn', d)], [('ps', D_[d])])
            yield
            if GDN_LEVEL < 10:
                return
            for d in range(2):
                n = blks[d]
                em.op('dve', lambda e, d=d, n=n: e.scalar_tensor_tensor(out=Sb[d][:], in0=Sf[d][:], scalar=egl[:, d, n:n + 1], in1=ps[D_[d]][:, 0:128], op0=OP.mult, op1=OP.add),
                      [('S', d), 'egl', ('ps', D_[d])], [('Sb', d)])
            yield
            for d in range(2):
                n = blks[d]
                em.op('dve', lambda e, d=d, n=n: e.scalar_tensor_tensor(out=Sf[d][:], in0=Sf[d][:], scalar=egl[:, d, n:n + 1], in1=ps[D_[d]][:, 0:128], op0=OP.mult, op1=OP.add),
                      [('S', d), 'egl', ('ps', D_[d])], [('S', d)])
                if i < NB // 2:
                    em.op('dve', lambda e, d=d, n=n: e.tensor_tensor(out=oacc[:, n, :], in0=ps[D_[d]][:, 128:256], in1=ot[d][:], op=OP.add), [('ps', D_[d]), ('ot', d)], [('oacc', n)])
                else:
                    em.op('dve', lambda e, d=d, n=n: e.tensor_tensor(out=ot[d][:], in0=ps[D_[d]][:, 128:256], in1=ot[d][:], op=OP.add), [('ps', D_[d]), ('ot', d)], [('ot', d)])
                    em.op('pool', lambda e, d=d, n=n: e.tensor_tensor(out=oacc[:, n, :], in0=oacc[:, n, :], in1=ot[d][:], op=OP.add), [('oacc', n), ('ot', d)], [('oacc', n)])
            yield

        def fin_step(i):
            blks = (i, NB - 1 - i)
            D_ = (6, 7)
            if i >= NB // 2 and GDN_LEVEL >= 11:
                for n in blks:
                    mg = n % 2
                    em.op('dve', lambda e, n=n: e.tensor_tensor(out=osq[:], in0=oacc[:, n, :], in1=oacc[:, n, :], op=OP.mult), [('oacc', n)], ['osq'])
                    em.op('dve', lambda e: e.tensor_reduce(out=ssq[:, 0:1], in_=osq[:], axis=mybir.AxisListType.X, op=OP.add), ['osq', 'ssq'], ['ssq'])
                    rsqrt(ssq[:, 1:2], ssq[:, 0:1], 1.0 / 128, ['ssq'], ['ssq'], ssq[:, 1:2], 'ssq')
                    em.op('dve', lambda e, n=n: e.tensor_scalar(out=on[:], in0=oacc[:, n, :], scalar1=ssq[:, 1:2], scalar2=None, op0=OP.mult), [('oacc', n), 'ssq'], ['on'])
                    em.op('pe', lambda e: e.matmul(ps[D_[0]][:, 256:384], lhsT=on[:], rhs=ident, start=True, stop=True), ['on', 'cb'], [('ps', D_[0])])
                    yield
                    em.op('act', lambda e, n=n, mg=mg: e.activation(out=on2[:], in_=ps[D_[0]][:, 256:384], func=AF.Copy), [('ps', D_[0])], ['on2'])
                    em.op('dve', lambda e, n=n, mg=mg: e.scalar_tensor_tensor(out=mixg[mg][:], in0=on2[:], scalar=onw_s[:, 0:1], in1=zs[:, n * 128:(n + 1) * 128], op0=OP.mult, op1=OP.mult),
                          ['on2', 'onw', 'zs'], [('mixg', mg)])
                    em.dma(mix_loc[h * 128:(h + 1) * 128, n * 128:(n + 1) * 128], mixg[mg][:], ('mixg', mg), [('mixg', mg)], [])
                    yield

        def run_interleaved(gens):
            gens = [g for g in gens if g is not None]
            while gens:
                for g in list(gens):
                    try:
                        next(g)
                    except StopIteration:
                        gens.remove(g)

        def seq(*gs):
            for g in gs:
                yield from g

        for j in range(NB // 2 + 1):
            tasks = []
            if j >= 1 and GDN_LEVEL >= 8:
                tasks.append(seq(chain_step(2 * j - 2), chain_step(2 * j - 1)))
            if j >= 2 and 2 * j - 4 >= NB // 2:
                tasks.append(seq(fin_step(2 * j - 4), fin_step(2 * j - 3)))
            if j < NB // 2:
                tasks += [pre_step(2 * j), pre_step(2 * j + 1)]
            run_interleaved(tasks)
        run_interleaved([seq(fin_step(NB - 2), fin_step(NB - 1))])

    stk['cur'].close()
    stk['cur'] = ExitStack()
    em.barrier()
    em.dma(mix_d[bass.ds(rank * 256, 256), :], mix_loc[0:256, :], 'xch', [], [], q='pool')
    em.dma(mix_d[bass.ds(rank * 256 + 512, 256), :], mix_loc[256:512, :], 'xch', [], [], q='pool')
    em.barrier()
    nc.all_core_barrier()
    em.dma(mix_in, mix_d[:, bass.ds(rank * TH, TH)], 'xch', [], [], q='pool')
    em.barrier()
    wo = sb("wo", [128, KC, D], BF)
    alloc_xs(2, 'o')
    xs = XS['b']
    for half in range(2):
        s = xs[half]
        em.dma(s[:], wout.rearrange("(kc p) n -> p kc n", p=128)[:, :, half * 512:(half + 1) * 512], ('xs', half), [], [('xs', half)])
        em.op('dve', lambda e: e.tensor_copy(out=wo[:, :, half * 512:(half + 1) * 512], in_=s[:]), [('xs', half)], ['wo'])
    mt = [sb("mt%d" % i, [128, KC, 512], BF) for i in range(2)]
    xr = [sb("xr%d" % i, [128, D], F32) for i in range(4)]
    mixv = mix_in.rearrange("(kc p) n -> p kc n", p=128)
    for n in range(TH // 128 if 'out' in STAGES else 0):
        t4, sub = n // 4, n % 4
        ms = t4 % 2
        sl = n % 4
        tk = slice(n * 128, (n + 1) * 128)
        if sub == 0:
            em.dma(mt[ms][:], mixv[:, :, t4 * 512:(t4 + 1) * 512], ('mt', ms), [], [('mt', ms)])
        em.dma(xr[sl][:], xtok[tk, :], ('xr', sl), [], [('xr', sl)])
        for half in range(2):
            bank = half + 2 * (n % 2)
            for kc in range(KC):
                em.op('pe', lambda e, kc=kc: e.matmul(ps[bank][:, :], lhsT=mt[ms][:, kc, sub * 128:(sub + 1) * 128], rhs=wo[:, kc, half * 512:(half + 1) * 512], start=(kc == 0), stop=(kc == KC - 1)),
                      [('mt', ms), 'wo'], [('ps', bank)], inc=(kc == KC - 1))
            em.op('dve', lambda e: e.tensor_tensor(out=xr[sl][:, half * 512:(half + 1) * 512], in0=ps[bank][:, :], in1=xr[sl][:, half * 512:(half + 1) * 512], op=OP.add),
                  [('ps', bank), ('xr', sl)], [('xr', sl)])
        em.dma(y[tk, :], xr[sl][:], ('xr', sl), [('xr', sl)], [])
    em.barrier()
    stk['cur'].close()
    return nc, es


def _consts():
    j = np.arange(128)[:, None]
    c = np.arange(128)[None, :]
    m = np.zeros((128, 12, 128), np.float32)
    m[:, 0] = (j <= c)
    m[:, 1] = (j >= c)
    m[:, 2] = (j > c)
    m[:, 3] = (j < c)
    m[:, 4] = (c >= j)
    m[:, 5] = (c <= j)
    m[:, 6] = (c > j)
    m[:, 7] = (c < j)
    for q_ in range(4):
        m[:, 8 + q_] = np.where(m[:, 4 + q_] > 0, 0.0, -30000.0)
    b = np.zeros((128, 6, 128), np.float32)
    b[:, 0] = np.eye(128)
    b[:, 1] = 1.0
    b[:, 2] = ((j // 64) == (c // 64))
    b[:, 3] = (j >= c)
    b[:, 4] = (j <= c)
    b[:, 5] = 1.0
    inv = 10000.0 ** (-np.arange(0, 64, 2, dtype=np.float32) / 64)
    ang = np.arange(T, dtype=np.float32)[:, None] * inv[None, :]
    ang = np.concatenate([ang, ang], -1)
    cos = np.cos(ang).T.astype(np.float32)
    sin = np.sin(ang).T.astype(np.float32)
    sin_s = np.concatenate([-sin[:32], sin[32:]], 0)
    rope = np.zeros((128, 2, T), np.float32)
    rope[:, 0] = np.concatenate([cos, cos], 0)
    rope[:, 1] = np.concatenate([sin_s, sin_s], 0)
    return m, b.astype(ml_dtypes.bfloat16), rope


_CACHE = {}


def kernel(x, norm_w, w_in, dn_conv_w, dn_a_log, dn_dt_bias, dn_out_norm_w,
           swa_q_norm_w, swa_k_norm_w, swa_sinks, w_out):
    x = np.asarray(x, np.float32)
    w = np.asarray(w_in, np.float32)[0]
    if 'nc' not in _CACHE:
        _CACHE['nc'] = build_program()
    nc, _ = _CACHE['nc']
    cm, cbf, rope = _consts()
    perm = np.concatenate([np.arange(32, 64), np.arange(0, 32)])
    o = 2064
    wq = w[:, o:o + 512]
    wk = w[:, o + 512:o + 640]
    wv = w[:, o + 640:o + 768]
    wz = w[:, o + 768:o + 1280]
    cwT = np.asarray(dn_conv_w, np.float32)[0].T.reshape(12, 128, 5)
    alog = np.asarray(dn_a_log, np.float32)[0]
    dtb = np.asarray(dn_dt_bias, np.float32)[0]
    qw = np.asarray(swa_q_norm_w, np.float32)[0]
    kw = np.asarray(swa_k_norm_w, np.float32)[0]
    qkw = np.stack([np.tile(qw, 2), np.tile(qw[perm], 2), np.tile(kw, 2), np.tile(kw[perm], 2)], 1).astype(np.float32)
    sk = np.asarray(swa_sinks, np.float32)[0]
    normw = np.ascontiguousarray(np.asarray(norm_w, np.float32)[0].reshape(KC, 128).T)
    onw = np.asarray(dn_out_norm_w, np.float32)[0].reshape(128, 1)
    per_rank = []
    for r in range(2):
        wg = np.zeros((2, D, 516), np.float32)
        convw = np.zeros((128, 6, 5), np.float32)
        gpar = np.zeros((128, 8), np.float32)
        for hh in range(2):
            h = 2 * r + hh
            for g in range(4):
                wg[hh, :, g * 128:(g + 1) * 128] = w[:, g * 512 + h * 128: g * 512 + (h + 1) * 128]
            wg[hh, :, 512] = w[:, 2048 + h]
            wg[hh, :, 513] = w[:, 2048 + 4 + h]
            wg[hh, :, 514] = w[:, 2056 + h]
            wg[hh, :, 515] = w[:, 2056 + 4 + h]
            for c3 in range(3):
                convw[:, c3 * 2 + hh, :] = cwT[c3 * 4 + h]
            for d in range(2):
                gpar[:, d * 2 + hh] = alog[d, h]
                gpar[:, 4 + d * 2 + hh] = dtb[d, h]
        kk = wk[:, r * 64:(r + 1) * 64]
        wswk = np.concatenate([kk, kk, kk[:, perm], kk[:, perm], wv[:, r * 64:(r + 1) * 64]], 1)
        wq_r = wq[:, r * 256:(r + 1) * 256]
        wswq = np.concatenate([wq_r, wq_r.reshape(D, 4, 64)[:, :, perm].reshape(D, 256), wz[:, r * 256:(r + 1) * 256]], 1)
        sinks = np.zeros((128, 2), np.float32)
        for cc in range(2):
            sinks[0:64, cc] = sk[2 * (2 * r + cc)]
            sinks[64:128, cc] = sk[2 * (2 * r + cc) + 1]
        per_rank.append(dict(wgdn=wg, convw=convw, gpar=gpar, wswk=np.ascontiguousarray(wswk), wswq=np.ascontiguousarray(wswq), sinks=sinks))
    common = dict(normw=normw, wout=np.asarray(w_out, np.float32)[0], onw=onw, qkw=qkw, cmask=cm, cbf=cbf, rope=rope)
    in_maps = []
    for c in range(NCORES):
        b, r = c // 2, c % 2
        m = dict(common)
        m.update(per_rank[r])
        m['xTh'] = np.ascontiguousarray(x[b, r * TH:(r + 1) * TH].T)
        m['xtok'] = np.ascontiguousarray(x[b, r * TH:(r + 1) * TH])
        in_maps.append(m)
    res = run_bass_kernel_spmd(nc, in_maps, core_ids=list(range(NCORES)))
    _CACHE['res'] = res
    out = np.zeros((2, T, D), np.float32)
    for c in range(NCORES):
        b, r = c // 2, c % 2
        out[b, r * TH:(r + 1) * TH] = np.asarray(res.results[c]['y'], np.float32)
    return out
```

```python
import numpy as np
import ml_dtypes
from contextlib import ExitStack
import concourse.bass as bass
import concourse.mybir as mybir
from concourse.bass_utils import run_bass_kernel_spmd

F32 = mybir.dt.float32
BF = mybir.dt.bfloat16
AF = mybir.ActivationFunctionType
OP = mybir.AluOpType

T = 8192
D = 1024
KC = 8
NT = 16
NB = 64
EPS = 1e-6
NLEV = 7


class Em:
    def __init__(self, nc, es):
        self.nc = nc
        self.eng = {'pe': nc.tensor, 'act': nc.scalar, 'dve': nc.vector, 'pool': nc.gpsimd, 'sp': nc.sync}
        self.csem = {e: es.enter_context(nc.semaphore('c_' + e)) for e in ('pe', 'act', 'dve', 'pool')}
        self.ccnt = {e: 0 for e in self.csem}
        self.es = es
        self.dsem = {}
        self.dcnt = {}
        self.res = {}
        self.waited = {e: {} for e in self.eng}
        self.pend = {e: ([], []) for e in self.eng}

    def _wait(self, e, ev):
        if ev[0] == 'c':
            if ev[1] == e and e == 'pe':
                return
            sem, val, k = self.csem[ev[1]], ev[2], ('c', ev[1])
        else:
            sem, val, k = self.dsem[ev[1]], 16 * self.dcnt[ev[1]], ('d', ev[1])
        if self.waited[e].get(k, 0) >= val:
            return
        self.waited[e][k] = val
        self.eng[e].wait_ge(sem, val)

    def deps(self, e, reads, writes):
        for r in reads:
            st = self.res.get(r)
            if st:
                for ev in st['w']:
                    self._wait(e, ev)
        for w in writes:
            st = self.res.get(w)
            if st:
                for ev in st['w']:
                    self._wait(e, ev)
                for ev in st['r'].values():
                    self._wait(e, ev)

    def record(self, ev, reads, writes):
        for r in reads:
            st = self.res.setdefault(r, {'w': [], 'r': {}})
            st['r'][ev[:2]] = ev
        for w in writes:
            self.res[w] = {'w': [ev], 'r': {}}

    def op(self, e, fn, reads, writes, inc=True):
        psr = [k for k in reads if isinstance(k, tuple) and k and k[0] == 'ps']
        if psr:
            reads = [k for k in reads if k not in psr]
            writes = list(writes) + [k for k in psr if k not in writes]
        self.deps(e, reads, writes)
        ins = fn(self.eng[e])
        if inc:
            self.ccnt[e] += 1
            ins.then_inc(self.csem[e], 1)
            pr, pw = self.pend[e]
            self.record(('c', e, self.ccnt[e]), list(reads) + pr, list(writes) + pw)
            self.pend[e] = ([], [])
        else:
            self.pend[e][0].extend(reads)
            self.pend[e][1].extend(writes)
        return ins

    def dma(self, out, in_, key, reads, writes, q='sp'):
        if key not in self.dsem:
            self.dsem[key] = self.es.enter_context(self.nc.semaphore('d_%d' % len(self.dsem)))
            self.dcnt[key] = 0
        self.deps(q, reads, writes)
        ins = self.eng[q].dma_start(out=out, in_=in_)
        self.dcnt[key] += 1
        ins.then_inc(self.dsem[key], 16)
        self.record(('d', key), reads, writes)

    def barrier(self):
        for e in self.eng:
            for k in self.dsem:
                self._wait(e, ('d', k))
            for c in self.csem:
                if self.ccnt[c]:
                    self._wait(e, ('c', c, self.ccnt[c]))


STAGES = {'s0', 'swa', 'gdn', 'out'}
NCORES = 4
GDN_HEADS = 2
TH = T // 2
GDN_LEVEL = 99
SWA_PARTS = {'kv', 'q', 'attn'}


def build_program():
    nc = bass.Bass("TRN2", target_bir_lowering=False, num_devices=NCORES)
    es = ExitStack()
    dram_in = {}

    def din(name, shape, dt=F32):
        dram_in[name] = nc.dram_tensor(name, list(shape), dt, kind="ExternalInput").ap()
        return dram_in[name]

    xT = din("xTh", [D, T // 2])
    xtok = din("xtok", [TH, D])
    normw = din("normw", [128, KC])
    wgdn = din("wgdn", [2, D, 516])
    wswk = din("wswk", [D, 320])
    wswq = din("wswq", [D, 768])
    wout = din("wout", [D, D])
    convw = din("convw", [128, 6, 5])
    gpar = din("gpar", [128, 8])
    onw = din("onw", [128, 1])
    qkw = din("qkw", [128, 4])
    esk = din("sinks", [128, 2])
    cmask = din("cmask", [128, 12, 128])
    cbf = din("cbf", [128, 6, 128], BF)
    rope = din("rope", [128, 2, T])
    y = nc.dram_tensor("y", [TH, D], F32, kind="ExternalOutput").ap()
    xb_d = nc.dram_tensor("xb_sh", [NT, 128, KC * 512], BF, addr_space="Shared").ap()
    _unused_doc = None
    rstd_d = nc.dram_tensor("rstd_sh", [128, T], F32, addr_space="Shared").ap()
    xb_l = nc.dram_tensor("xb_l", [NT // 2, 128, KC * 512], BF).ap()
    rstd_l = nc.dram_tensor("rstd_l", [128, T // 2], F32).ap()
    mix_d = nc.dram_tensor("mix_sh", [D, T], BF, addr_space="Shared").ap()
    mix_loc = nc.dram_tensor("mix_loc", [512, T], BF).ap()
    mix_in = nc.dram_tensor("mix_in", [D, TH], BF).ap()
    rank = nc.gpsimd.partition_id() % 2

    em = Em(nc, es)

    stk = {'cur': es}

    def sb(name, shape, dt):
        return stk['cur'].enter_context(nc.sbuf_tensor(name, list(shape), dt))

    WS = {}

    ps = [es.enter_context(nc.psum_tensor("ps%d" % i, [128, 512], F32)) for i in range(8)]

    mk = sb("mk", [128, 12, 128], F32)
    cb = sb("cb", [128, 6, 128], BF)
    nw = sb("nw", [128, KC], F32)
    cw = sb("cw", [128, 6, 5], F32)
    gp = sb("gp", [128, 8], F32)
    onw_s = sb("onw_s", [128, 1], F32)
    qkw_s = sb("qkw_s", [128, 4], F32)
    esk_s = sb("esk_s", [128, 2], F32)
    for dst, src, k in ((mk, cmask, 'mk'), (cb, cbf, 'cb'), (nw, normw, 'nw'), (cw, convw, 'cw'), (gp, gpar, 'gp'),
                        (onw_s, onw, 'onw'), (qkw_s, qkw, 'qkw'), (esk_s, esk, 'esk')):
        em.dma(dst[:], src, 'const', [], [k])
    M_CUMF, M_CUMB, M_ASF, M_ASB, M_TIF, M_TIB, M_TSF, M_TSB = range(8)
    ident = cb[:, 0, :]
    ones_bf = cb[:, 1, :]
    blk1 = cb[:, 2, :]
    identf = sb("identf", [128, 128], F32)
    onesf = sb("onesf", [128, 128], F32)
    em.op('dve', lambda e: e.tensor_copy(out=identf[:], in_=ident), ['cb'], ['identf'])
    em.op('dve', lambda e: e.tensor_copy(out=onesf[:], in_=ones_bf), ['cb'], ['onesf'])
    ealog = sb("ealog", [128, 4], F32)
    em.op('act', lambda e: e.activation(out=ealog[:], in_=gp[:, 0:4], func=AF.Exp), ['gp'], ['ealog'])
    em.op('act', lambda e: e.activation(out=esk_s[:], in_=esk_s[:], func=AF.Exp), ['esk'], ['esk'])

    XS = {}

    def alloc_xs(n, tag):
        XS['b'] = [sb("xs%s%d" % (tag, i), [128, KC, 512], F32) for i in range(n)]
    xbt = [sb("xbt%d" % i, [128, KC, 512], BF) for i in range(2)]
    rst = [sb("rst%d" % i, [128, 512], F32) for i in range(2)]
    tA = sb("tA", [128, 512], F32)

    def rsqrt(out_ap, in_ap, scale, rd, wr, tmp, tmpk):
        em.op('act', lambda e: e.activation(out=tmp, in_=in_ap, func=AF.Ln, bias=epsb[:, 0:1], scale=scale), rd + ['epsb'], [tmpk])
        em.op('act', lambda e: e.activation(out=out_ap, in_=tmp, func=AF.Exp, scale=-0.5), [tmpk], wr)

    epsb = sb("epsb", [128, 1], F32)
    em.op('dve', lambda e: e.memset(epsb[:], EPS), [], ['epsb'])

    def load_weights(src_ap, ncols, extra_w=()):
        c0 = 0
        i = 0
        while c0 < ncols:
            w = min(512, ncols - c0)
            nb_ = len(XS['b'])
            s = XS['b'][i % nb_]
            em.dma(s[:, :, 0:w], src_ap.rearrange("(kc p) n -> p kc n", p=128)[:, :, c0:c0 + w], ('xs', i % nb_), [], [('xs', i % nb_)] + list(extra_w))
            for kc in range(KC):
                em.op('dve', lambda e, kc=kc, s=s, c0=c0, w=w: e.tensor_scalar(out=WS['w'][:, kc, c0:c0 + w], in0=s[:, kc, 0:w], scalar1=nw[:, kc:kc + 1], scalar2=None, op0=OP.mult),
                      [('xs', i % nb_), 'nw'], ['wsb'])
            c0 += w
            i += 1

    stk['cur'] = ExitStack()
    sq = sb("sq", [128, KC, 512], BF)
    alloc_xs(2, 'p')
    xs = XS['b']
    xTv = xT.rearrange("(kc p) n -> p kc n", p=128)
    for t in range(NT // 2):
        sl = t % 2
        tok = slice(t * 512, (t + 1) * 512)
        em.dma(xs[sl][:], xTv[:, :, tok], ('xs', sl), [], [('xs', sl)])
        em.op('act', lambda e: e.activation(out=sq[:], in_=xs[sl][:], func=AF.Square), [('xs', sl)], ['sq'])
        em.op('dve', lambda e: e.tensor_copy(out=xbt[sl][:], in_=xs[sl][:]), [('xs', sl)], [('xbt', sl)])
        for kc in range(KC):
            em.op('pe', lambda e, kc=kc: e.matmul(ps[0][:, :], lhsT=ones_bf, rhs=sq[:, kc, :], start=(kc == 0), stop=(kc == KC - 1)),
                  ['sq', 'cb'], [('ps', 0)], inc=(kc == KC - 1))
        rsqrt(rst[sl][:], ps[0][:, :], 1.0 / D, [('ps', 0)], [('rst', sl)], tA[:], 'tA')
        em.dma(xb_l[t].rearrange("p (kc n) -> p kc n", kc=KC), xbt[sl][:], ('xbt', sl), [('xbt', sl)], [])
        em.dma(rstd_l[:, tok], rst[sl][:], ('rst', sl), [('rst', sl)], [])
    em.barrier()
    em.dma(xb_d[bass.ds(rank * (NT // 2), NT // 2), :, :], xb_l, 'xch0', [], [], q='pool')
    em.dma(rstd_d[:, bass.ds(rank * (T // 2), T // 2)], rstd_l, 'xch0', [], [], q='pool')
    em.barrier()
    nc.all_core_barrier()
    stk['cur'].close()
    stk['cur'] = es

    def load_tile(t):
        sl = t % 2
        tok = slice(t * 512, (t + 1) * 512)
        em.dma(xbt[sl][:], xb_d[t].rearrange("p (kc n) -> p kc n", kc=KC), ('xbt', sl), [], [('xbt', sl)])
        em.dma(rst[sl][:], rstd_d[:, tok], ('rst', sl), [], [('rst', sl)])
        return sl

    def project(sl, c0, m, bank):
        for kc in range(KC):
            em.op('pe', lambda e, kc=kc: e.matmul(ps[bank][0:m, :], lhsT=WS['w'][:, kc, c0:c0 + m], rhs=xbt[sl][:, kc, :], start=(kc == 0), stop=(kc == KC - 1)),
                  ['wsb', ('xbt', sl)], [('ps', bank)], inc=(kc == KC - 1))

    SWA_ON = 'swa' in STAGES
    stk['cur'] = ExitStack()
    WS['w'] = sb("wsb_swa", [128, KC, 768], BF)
    alloc_xs(2, 's')
    kAB = sb("kAB", [128, 1, T], BF)
    vpad = sb("vpad", [128, NB, 1, 192], BF)
    opad = sb("opad", [128, 192], BF)
    em.op('dve', lambda e: e.memset(vpad[:], 0.0), [], ['vpad'])
    em.op('dve', lambda e: e.memset(opad[:], 0.0), [], ['opad'])
    em.op('dve', lambda e: e.memset(opad[:, 64:128], 1.0), ['opad'], ['opad'])
    rp = [sb("rp%d" % i, [128, 2, 512], F32) for i in range(2)]
    smask = sb("smask", [128, 384], BF)
    em.op('dve', lambda e: e.tensor_copy(out=smask[:, 0:128], in_=cb[:, 3, :]), ['cb'], ['smask'])
    em.op('dve', lambda e: e.tensor_copy(out=smask[:, 128:256], in_=cb[:, 1, :]), ['cb', 'smask'], ['smask'])
    em.op('dve', lambda e: e.tensor_copy(out=smask[:, 256:384], in_=cb[:, 4, :]), ['cb', 'smask'], ['smask'])
    NSET = 2
    uA = [sb("uA%d" % s, [128, 512], F32) for s in range(NSET)]
    uB = [sb("uB%d" % s, [128, 512], F32) for s in range(NSET)]
    uC = [sb("uC%d" % s, [128, 512], F32) for s in range(NSET)]
    uD = [sb("uD%d" % s, [128, 512], BF) for s in range(NSET)]

    def run_interleaved(gens):
        gens = [g for g in gens if g is not None]
        while gens:
            for g in list(gens):
                try:
                    next(g)
                except StopIteration:
                    gens.remove(g)

    def qk_chain(sl, s, c0, c0rot, wcol, out_ap, outk, bA, bB):
        A, B_, C, Dq = uA[s], uB[s], uC[s], uD[s]
        kA, kB_, kC, kD = ('uA', s), ('uB', s), ('uC', s), ('uD', s)
        project(sl, c0, 128, bA)
        project(sl, c0rot, 128, bB)
        yield
        em.op('dve', lambda e: e.tensor_tensor(out=A[:], in0=ps[bA][:, :], in1=rst[sl][:], op=OP.mult), [('ps', bA), ('rst', sl)], [kA])
        em.op('act', lambda e: e.activation(out=Dq[:], in_=A[:], func=AF.Square), [kA], [kD])
        em.op('dve', lambda e: e.tensor_tensor(out=B_[:], in0=ps[bB][:, :], in1=rst[sl][:], op=OP.mult), [('ps', bB), ('rst', sl)], [kB_])
        yield
        em.op('pe', lambda e: e.matmul(ps[bA][:, :], lhsT=blk1, rhs=Dq[:], start=True, stop=True), [kD, 'cb'], [('ps', bA)])
        yield
        em.op('act', lambda e: e.activation(out=C[:], in_=ps[bA][:, :], func=AF.Ln, bias=epsb[:, 0:1], scale=1.0 / 64), [('ps', bA), 'epsb'], [kC])
        yield
        em.op('act', lambda e: e.activation(out=C[:], in_=C[:], func=AF.Exp, scale=-0.5), [kC], [kC])
        em.op('dve', lambda e: e.scalar_tensor_tensor(out=A[:], in0=A[:], scalar=qkw_s[:, wcol:wcol + 1], in1=C[:], op0=OP.mult, op1=OP.mult), [kA, kC, 'qkw'], [kA])
        em.op('dve', lambda e: e.scalar_tensor_tensor(out=B_[:], in0=B_[:], scalar=qkw_s[:, wcol + 1:wcol + 2], in1=C[:], op0=OP.mult, op1=OP.mult), [kB_, kC, 'qkw'], [kB_])
        yield
        em.op('pool', lambda e: e.tensor_tensor(out=A[:], in0=A[:], in1=rp[sl][:, 0, :], op=OP.mult), [kA, ('rp', sl)], [kA])
        em.op('dve', lambda e: e.tensor_tensor(out=B_[:], in0=B_[:], in1=rp[sl][:, 1, :], op=OP.mult), [kB_, ('rp', sl)], [kB_])
        yield
        em.op('dve', lambda e: e.tensor_tensor(out=out_ap, in0=A[:], in1=B_[:], op=OP.add), [kA, kB_], [outk])
        yield

    def load_tile_rope(t):
        sl = load_tile(t)
        em.dma(rp[sl][:], rope[:, :, t * 512:(t + 1) * 512], ('rp', sl), [], [('rp', sl)])
        return sl

    load_weights(wswk, 320)

    def kv_tile(t):
        s = t % 2
        sl = load_tile_rope(t)
        tok = slice(t * 512, (t + 1) * 512)
        yield from qk_chain(sl, s, 0, 128, 2, kAB[:, 0, tok], 'kAB', 2 * s, 2 * s + 1)
        bV, bT = 4 + 2 * s, 5 + 2 * s
        project(sl, 256, 64, bV)
        yield
        em.op('dve', lambda e: e.tensor_tensor(out=uD[s][0:64, :], in0=ps[bV][0:64, :], in1=rst[sl][0:64, :], op=OP.mult), [('ps', bV), ('rst', sl)], [('uD', s)])
        yield
        for j in range(4):
            em.op('pe', lambda e, j=j: e.matmul(ps[bT][:, j * 64:(j + 1) * 64], lhsT=uD[s][0:64, j * 128:(j + 1) * 128], rhs=cb[0:64, 0, 0:64], start=True, stop=True),
                  [('uD', s), 'cb'], [('ps', bT)], inc=(j == 3))
        yield
        em.op('act', lambda e: e.activation(out=vpad[:, t * 4:(t + 1) * 4, 0, 64:128], in_=ps[bT][:, 0:256].rearrange("p (j c) -> p j c", j=4), func=AF.Copy),
              [('ps', bT)], ['vpad'])
        yield

    for t in range(0, NT if (SWA_ON and 'kv' in SWA_PARTS) else 0, 2):
        run_interleaved([kv_tile(t), kv_tile(t + 1)])

    load_weights(wswq, 768)
    qr = [sb("qr%d" % i, [128, 2, 512], BF) for i in range(2)]
    zs4 = [sb("zs4%d" % i, [128, 2, 512], BF) for i in range(2)]
    pT = [[sb("pT%d_%d" % (g, i), [128, 384], BF) for i in range(2)] for g in range(2)]
    mixo = [sb("mixo%d" % i, [128, 512], BF) for i in range(2)]
    fA = sb("fA", [128, 512], F32)

    def q_tile(t):
        qb = t % 2
        sl = load_tile_rope(t)
        for c in range(2):
            s = c % NSET
            bA = c % 2
            project(sl, 512 + c * 128, 128, bA)
            yield
            em.op('dve', lambda e: e.tensor_tensor(out=uA[s][:], in0=ps[bA][:, :], in1=rst[sl][:], op=OP.mult), [('ps', bA), ('rst', sl)], [('uA', s)])
            yield
            em.op('act', lambda e, c=c: e.activation(out=zs4[qb][:, c, :], in_=uA[s][:], func=AF.Silu), [('uA', s)], [('zs4', qb)])
            yield
        for c in range(2):
            yield from qk_chain(sl, c % NSET, c * 128, 256 + c * 128, 0, qr[qb][:, c, :], ('qr', qb), 0, 1)

    def attn_tile(t):
        qb = t % 2
        tok = slice(t * 512, (t + 1) * 512)
        groups = [(c, q4) for c in range(2) for q4 in range(4)]

        def scores(gi):
            c, q4 = groups[gi]
            i = t * 4 + q4
            js = [j for j in (i - 1, i, i + 1) if 0 <= j < NB]
            gp_ = gi % 2
            for hh in range(2):
                prt = slice(hh * 64, hh * 64 + 64)
                bank = 2 + 2 * gp_ + hh
                pt = pT[gp_][hh]
                for j in js:
                    so = (j - i + 1) * 128
                    em.op('pe', lambda e, j=j, so=so: e.matmul(ps[bank][:, so:so + 128], lhsT=kAB[prt, 0, j * 128:(j + 1) * 128],
                                                             rhs=qr[qb][prt, c, q4 * 128:(q4 + 1) * 128], start=True, stop=True),
                          ['kAB', ('qr', qb)], [('ps', bank)], inc=(j == js[-1]))
                lo, hi = (js[0] - i + 1) * 128, (js[-1] - i + 2) * 128
                em.op('act', lambda e: e.activation(out=pt[:, lo:hi], in_=ps[bank][:, lo:hi], func=AF.Exp, scale=0.125), [('ps', bank)], [('pT', gp_, hh)])
                meng = 'dve' if hh == 0 else 'pool'
                em.op(meng, lambda e: e.tensor_tensor(out=pt[:, lo:hi], in0=pt[:, lo:hi], in1=smask[:, lo:hi], op=OP.mult), [('pT', gp_, hh), 'smask'], [('pT', gp_, hh)])

        def pv(gi):
            c, q4 = groups[gi]
            i = t * 4 + q4
            js = [j for j in (i - 1, i, i + 1) if 0 <= j < NB]
            gp_ = gi % 2
            for which in range(2):
                bank = 6 + which
                n = 0
                for hh in range(2):
                    for j in js:
                        so = (j - i + 1) * 128
                        win = slice(64, 192) if hh == 0 else slice(0, 128)
                        lhs = vpad[:, j, 0, win] if which == 0 else opad[:, win]
                        n += 1
                        em.op('pe', lambda e, lhs=lhs, so=so, hh=hh, n=n: e.matmul(ps[bank][:, q4 * 128:(q4 + 1) * 128], lhsT=lhs, rhs=pT[gp_][hh][:, so:so + 128],
                                                                                 start=(n == 1), stop=(n == 2 * len(js))),
                              ['vpad', 'opad', ('pT', gp_, 0), ('pT', gp_, 1)], [('ps', bank)], inc=(n == 2 * len(js)))
            if q4 == 3:
                msl = c % 2
                em.op('act', lambda e: e.activation(out=fA[:], in_=ps[7][:, :], func=AF.Ln, bias=esk_s[:, c:c + 1], scale=1.0), [('ps', 7), 'esk'], ['fA'])
                em.op('act', lambda e: e.activation(out=fA[:], in_=fA[:], func=AF.Exp, scale=-1.0), ['fA'], ['fA'])
                em.op('dve', lambda e: e.tensor_tensor(out=fA[:], in0=ps[6][:, :], in1=fA[:], op=OP.mult), [('ps', 6), 'fA'], ['fA'])
                em.op('dve', lambda e: e.tensor_tensor(out=mixo[msl][:], in0=fA[:], in1=zs4[qb][:, c, :], op=OP.mult), ['fA', ('zs4', qb)], [('mixo', msl)])
                em.dma(mix_loc[256 + c * 128:256 + (c + 1) * 128, tok], mixo[msl][:], ('mixo', msl), [('mixo', msl)], [])

        for gi in range(len(groups) + 1):
            if gi < len(groups):
                scores(gi)
                yield
            if gi >= 1:
                pv(gi - 1)
                yield

    for t in range(NT + 1 if SWA_ON else 0):
        run_interleaved([attn_tile(t - 1) if (t > 0 and 'attn' in SWA_PARTS) else None, q_tile(t) if (t < NT and 'q' in SWA_PARTS) else None])
    em.barrier()
    stk['cur'].close()
    stk['cur'] = ExitStack()
    WS['w'] = sb("wsb_gdn", [128, KC, 528], BF)

    qk = sb("qk", [128, NB, 2, 128], BF)
    ktok = sb("ktok", [128, NB, 128], BF)
    vtok = sb("vtok", [128, NB, 128], BF)
    zs = sb("zs", [128, T], BF)
    oacc = sb("oacc", [128, NB, 128], F32)
    XS['b'] = [oacc[:, 0:32, :].rearrange("p (a b) c -> p a (b c)", a=8, b=4)]
    pre = [sb("pre%d" % i, [128, 3, 516], BF) for i in range(3)]
    wdiag = sb("wdiag", [128, 15, 128], BF)
    bdT = sb("bdT", [4, 512], F32)
    graw = sb("graw", [128, NB, 4], F32)
    gt = sb("gt", [128, 14, NB], F32)
    egl = sb("egl", [128, 2, NB], F32)
    Bm = [[sb("Bm%d_%d" % (s, d), [128, 128], F32) for d in range(2)] for s in range(2)]
    EE = [[sb("EE%d_%d" % (s, d), [128, 256], F32) for d in range(2)] for s in range(2)]
    E1 = [[sb("E1%d_%d" % (s, d), [128, 128], F32) for d in range(2)] for s in range(2)]
    Wb = [[sb("Wb%d_%d" % (d, p), [128, 512], BF) for p in range(4)] for d in range(2)]
    osq = sb("osq", [128, 128], F32)
    cA = [sb("cA%d" % s, [128, 512], F32) for s in range(2)]
    cC = [sb("cC%d" % s, [128, 512], F32) for s in range(2)]
    cD = [sb("cD%d" % s, [128, 512], BF) for s in range(2)]
    pz = sb("pz", [128, 512], F32)
    on2 = sb("on2", [128, 128], F32)
    Sf = [sb("Sf%d" % d, [128, 128], F32) for d in range(2)]
    Sb = [sb("Sb%d" % d, [128, 128], BF) for d in range(2)]
    rr = [sb("rr%d" % d, [128, 128], BF) for d in range(2)]
    vn = [sb("vn%d" % d, [128, 128], BF) for d in range(2)]
    kd = [sb("kd%d" % d, [128, 128], BF) for d in range(2)]
    ot = [sb("ot%d" % d, [128, 128], F32) for d in range(2)]
    on = sb("on", [128, 128], BF)
    ssq = sb("ssq", [128, 2], F32)
    mixg = [sb("mixg%d" % i, [128, 128], BF) for i in range(2)]

    for h in range(GDN_HEADS if 'gdn' in STAGES else 0):
        load_weights(wgdn[h], 516, extra_w=[('oacc', n) for n in range(NB)])
        for c3 in range(3):
            for tap in range(5):
                em.op('dve', lambda e, c3=c3, tap=tap: e.tensor_scalar(out=wdiag[:, c3 * 5 + tap, :], in0=identf[:], scalar1=cw[:, c3 * 2 + h, tap:tap + 1], scalar2=None, op0=OP.mult),
                      ['identf', 'cw'], ['wdiag'])
        for s3 in range(3):
            em.op('dve', lambda e, s3=s3: e.memset(pre[s3][:], 0.0), [], [('pre', s3)])

        def conv_tile(tt):
            s3 = tt % 3
            for c3 in range(3):
                s = c3 % 2
                cb_ = 4 + s
                A, C, Dq = cA[s], cC[s], cD[s]
                kA, kC, kD = ('cA', s), ('cC', s), ('cD', s)
                for tap in range(5):
                    em.op('pe', lambda e, c3=c3, tap=tap: e.matmul(ps[cb_][:, :], lhsT=wdiag[:, c3 * 5 + tap, :], rhs=pre[s3][:, c3, tap:tap + 512], start=(tap == 0), stop=(tap == 4)),
                          ['wdiag', ('pre', s3)], [('ps', cb_)], inc=(tap == 4))
                yield
                em.op('act', lambda e: e.activation(out=A[:], in_=ps[cb_][:, :], func=AF.Silu), [('ps', cb_)], [kA])
                if c3 < 2:
                    em.op('act', lambda e: e.activation(out=Dq[:], in_=A[:], func=AF.Square), [kA], [kD])
                    yield
                    em.op('pe', lambda e: e.matmul(ps[cb_][:, :], lhsT=ones_bf, rhs=Dq[:], start=True, stop=True), [kD, 'cb'], [('ps', cb_)])
                    yield
                    em.op('act', lambda e: e.activation(out=C[:], in_=ps[cb_][:, :], func=AF.Ln, bias=epsb[:, 0:1], scale=1.0), [('ps', cb_), 'epsb'], [kC])
                    em.op('act', lambda e: e.activation(out=C[:], in_=C[:], func=AF.Exp, scale=-0.5), [kC], [kC])
                    scl = (128.0 ** -0.5) if c3 == 0 else 1.0
                    em.op('dve', lambda e, c3=c3, scl=scl: e.scalar_tensor_tensor(out=qk[:, tt * 4:(tt + 1) * 4, c3, :], in0=A[:].rearrange("p (j c) -> p j c", j=4), scalar=scl,
                                                                                in1=C[:].rearrange("p (j c) -> p j c", j=4), op0=OP.mult, op1=OP.mult), [kA, kC], ['qk'])
                    yield
                    if c3 == 1:
                        for j in range(4):
                            em.op('pe', lambda e, j=j: e.matmul(ps[6][:, j * 128:(j + 1) * 128], lhsT=qk[:, tt * 4 + j, 1, :], rhs=ident, start=True, stop=True),
                                  ['qk', 'cb'], [('ps', 6)], inc=(j == 3))
                        yield
                        em.op('act', lambda e: e.activation(out=ktok[:, tt * 4:(tt + 1) * 4, :], in_=ps[6][:, :].rearrange("p (j c) -> p j c", j=4), func=AF.Copy), [('ps', 6)], ['ktok'])
                else:
                    em.op('dve', lambda e: e.tensor_copy(out=Dq[:], in_=A[:]), [kA], [kD])
                    yield
                    for j in range(4):
                        em.op('pe', lambda e, j=j: e.matmul(ps[6][:, j * 128:(j + 1) * 128], lhsT=Dq[:, j * 128:(j + 1) * 128], rhs=ident, start=True, stop=True),
                              [kD, 'cb'], [('ps', 6)], inc=(j == 3))
                    yield
                    em.op('act', lambda e: e.activation(out=vtok[:, tt * 4:(tt + 1) * 4, :], in_=ps[6][:, :].rearrange("p (j c) -> p j c", j=4), func=AF.Copy), [('ps', 6)], ['vtok'])
                yield

        def proj_tile(t):
            sl = load_tile(t)
            s3 = t % 3
            if t >= 2:
                em.op('dve', lambda e: e.memset(pre[s3][:, :, 514:516], 0.0), [('pre', s3)], [('pre', s3)])
            for c3 in range(3):
                project(sl, c3 * 128, 128, c3)
                yield
                em.op('dve', lambda e, c3=c3: e.tensor_tensor(out=pre[s3][:, c3, 2:514], in0=ps[c3][:, :], in1=rst[sl][:], op=OP.mult), [('ps', c3), ('rst', sl)], [('pre', s3)])
            if t > 0:
                sp_ = (t - 1) % 3
                em.op('act', lambda e: e.activation(out=pre[sp_][:, :, 514:516], in_=pre[s3][:, :, 2:4], func=AF.Copy), [('pre', s3), ('pre', sp_)], [('pre', sp_)])
                em.op('act', lambda e: e.activation(out=pre[s3][:, :, 0:2], in_=pre[sp_][:, :, 512:514], func=AF.Copy), [('pre', s3), ('pre', sp_)], [('pre', s3)])
            project(sl, 384, 128, 3)
            yield
            em.op('dve', lambda e: e.tensor_tensor(out=pz[:], in0=ps[3][:, :], in1=rst[sl][:], op=OP.mult), [('ps', 3), ('rst', sl)], ['pz'])
            em.op('act', lambda e: e.activation(out=zs[:, t * 512:(t + 1) * 512], in_=pz[:], func=AF.Silu), ['pz'], ['zs'])
            project(sl, 512, 4, 7)
            yield
            em.op('dve', lambda e: e.tensor_tensor(out=bdT[:], in0=ps[7][0:4, :], in1=rst[sl][0:4, :], op=OP.mult), [('ps', 7), ('rst', sl)], ['bdT'])
            yield
            for j in range(4):
                em.op('pe', lambda e, j=j: e.matmul(ps[7][:, j * 4:(j + 1) * 4], lhsT=bdT[0:4, j * 128:(j + 1) * 128], rhs=identf[0:4, 0:4], start=True, stop=True),
                      ['bdT', 'identf'], [('ps', 7)], inc=(j == 3))
            yield
            em.op('act', lambda e: e.activation(out=graw[:, t * 4:(t + 1) * 4, :], in_=ps[7][:, 0:16].rearrange("p (j c) -> p j c", j=4), func=AF.Copy), [('ps', 7)], ['graw'])
            yield

        for t in range(NT + 2):
            run_interleaved([proj_tile(t) if t < NT else None, conv_tile(t - 2) if (t >= 2 and GDN_LEVEL >= 2) else None])

        for d in range(2 if GDN_LEVEL >= 3 else 0):
            em.op('act', lambda e: e.activation(out=gt[:, d, :], in_=graw[:, :, d], func=AF.Exp, scale=-1.0), ['graw'], ['gt'])
            em.op('dve', lambda e: e.tensor_scalar(out=gt[:, d, :], in0=gt[:, d, :], scalar1=1.0, scalar2=None, op0=OP.add), ['gt'], ['gt'])
            em.op('dve', lambda e: e.reciprocal(out=gt[:, d, :], in_=gt[:, d, :]), ['gt'], ['gt'])
            em.op('dve', lambda e: e.tensor_scalar(out=gt[:, 2 + d, :], in0=gt[:, d, :], scalar1=-1.0, scalar2=None, op0=OP.mult), ['gt'], ['gt'])
            em.op('act', lambda e: e.activation(out=gt[:, 4 + d, :], in_=graw[:, :, 2 + d], func=AF.Exp, bias=gp[:, 4 + d * 2 + h:5 + d * 2 + h], scale=1.0), ['graw', 'gp'], ['gt'])
            em.op('act', lambda e: e.activation(out=gt[:, 4 + d, :], in_=gt[:, 4 + d, :], func=AF.Ln, bias=1.0, scale=1.0), ['gt'], ['gt'])
            em.op('dve', lambda e: e.tensor_scalar(out=gt[:, 4 + d, :], in0=gt[:, 4 + d, :], scalar1=ealog[:, d * 2 + h:d * 2 + h + 1], scalar2=-1.0, op0=OP.mult, op1=OP.mult), ['gt', 'ealog'], ['gt'])
            em.op('pe', lambda e: e.matmul(ps[0][:, 0:NB], lhsT=mk[:, M_CUMF + d, :], rhs=gt[:, 4 + d, :], start=True, stop=True), ['gt', 'mk'], [('ps', 0)])
            em.op('pe', lambda e: e.matmul(ps[1][:, 0:NB], lhsT=onesf[:], rhs=gt[:, 4 + d, :], start=True, stop=True), ['gt', 'onesf'], [('ps', 1)])
            em.op('act', lambda e: e.activation(out=gt[:, 6 + d, :], in_=ps[0][:, 0:NB], func=AF.Exp), [('ps', 0)], ['gt'])
            em.op('dve', lambda e: e.tensor_scalar(out=gt[:, 8 + d, :], in0=gt[:, 6 + d, :], scalar1=-1.0, scalar2=None, op0=OP.mult), ['gt'], ['gt'])
            em.op('act', lambda e: e.activation(out=egl[:, d, :], in_=ps[1][:, 0:NB], func=AF.Exp), [('ps', 1)], ['egl'])
            em.op('act', lambda e: e.activation(out=tA[:, 0:NB], in_=ps[0][:, 0:NB], func=AF.Copy), [('ps', 0)], ['src'])
            em.op('dve', lambda e: e.tensor_tensor(out=tA[:, 0:NB], in0=ps[1][:, 0:NB], in1=tA[:, 0:NB], op=OP.subtract), [('ps', 1), 'src'], ['src'])
            em.op('act', lambda e: e.activation(out=gt[:, 10 + d, :], in_=tA[:, 0:NB], func=AF.Exp), ['src'], ['gt'])
            em.op('dve', lambda e: e.tensor_tensor(out=gt[:, 12 + d, :], in0=gt[:, 10 + d, :], in1=gt[:, d, :], op=OP.mult), ['gt'], ['gt'])
            em.op('dve', lambda e: e.memset(Sf[d][:], 0.0), [], [('S', d)])
            em.op('dve', lambda e: e.memset(Sb[d][:], 0.0), [], [('Sb', d)])

        def pre_step(i):
            sl2 = i % 2
            p4 = i % 4
            blks = (i, NB - 1 - i)
            BK = (sl2 * 2, sl2 * 2 + 1)
            M_NEGI, M_NEGS = 8, 10
            kB = lambda d: ('Bm', sl2, d)
            kE = lambda d: ('EE', sl2, d)
            kW = lambda d: ('Wb', d, p4)
            MM = dict(skip_group_check=True)
            for d in range(2):
                n = blks[d]
                bk = ps[BK[d]]
                em.op('dve', lambda e, d=d, n=n: e.tensor_scalar(out=Bm[sl2][d][:], in0=mk[:, M_CUMF + d, :], scalar1=gt[:, 4 + d, n:n + 1], scalar2=None, op0=OP.mult), ['mk', 'gt'], [kB(d)])
                em.op('pe', lambda e, d=d, bk=bk: e.matmul(bk[:, 0:128], lhsT=mk[:, M_ASF + d, :], rhs=Bm[sl2][d][:], start=True, stop=True, **MM), ['mk', kB(d)], [('ps', BK[d])], inc=False)
                em.op('pe', lambda e, d=d, n=n, bk=bk: e.matmul(bk[:, 256:512], lhsT=qk[:, n, 1, :], rhs=qk[:, n, :, :], start=False, stop=True, **MM), ['qk'], [('ps', BK[d])])
            yield
            if GDN_LEVEL < 5:
                return
            for d in range(2):
                em.op('act', lambda e, d=d: e.activation(out=E1[sl2][d][:], in_=ps[BK[d]][:, 0:128], func=AF.Exp), [('ps', BK[d])], [('E1', sl2, d)])
                em.op('pool', lambda e, d=d: e.tensor_tensor(out=EE[sl2][d][:, 0:128], in0=E1[sl2][d][:], in1=mk[:, M_TIF + d, :], op=OP.mult), [('E1', sl2, d), 'mk'], [kE(d)])
                em.op('pool', lambda e, d=d: e.tensor_tensor(out=EE[sl2][d][:, 128:256], in0=E1[sl2][d][:], in1=mk[:, M_TSF + d, :], op=OP.mult), [('E1', sl2, d), 'mk', kE(d)], [kE(d)])
            yield
            if GDN_LEVEL < 6:
                return
            for d in range(2):
                n = blks[d]
                W = Wb[d][p4]
                em.op('dve', lambda e, d=d, n=n, W=W: e.scalar_tensor_tensor(out=W[:, 0:256], in0=ps[BK[d]][:, 256:512], scalar=gt[:, d, n:n + 1], in1=EE[sl2][d][:], op0=OP.mult, op1=OP.mult),
                      [('ps', BK[d]), kE(d), 'gt'], [kW(d)])
                em.op('pe', lambda e, d=d, W=W: e.matmul(ps[BK[d]][:, 256:384], lhsT=W[:, 128:256], rhs=ident, start=True, stop=True, **MM), [kW(d), 'cb'], [('ps', BK[d])])
                em.op('pool', lambda e, d=d, W=W: e.tensor_tensor(out=W[:, 256:384], in0=ident, in1=W[:, 128:256], op=OP.subtract), [kW(d), 'cb'], [kW(d)])
            yield
            for d in range(2):
                W = Wb[d][p4]
                em.op('act', lambda e, d=d, W=W: e.activation(out=W[:, 384:512], in_=ps[BK[d]][:, 256:384], func=AF.Copy), [('ps', BK[d])], [kW(d)])
            yield
            if GDN_LEVEL < 7:
                return
            for m in range(NLEV):
                lastm = (m == NLEV - 1)
                for d in range(2):
                    W = Wb[d][p4]
                    bk = ps[BK[d]]
                    P, R, PT = W[:, 128:256], W[:, 256:384], W[:, 384:512]
                    if m == 0:
                        em.op('pe', lambda e, bk=bk, P=P, PT=PT: e.matmul(bk[:, 0:128], lhsT=PT, rhs=P, start=True, stop=True, **MM), [kW(d)], [('ps', BK[d])], inc=False)
                        em.op('pe', lambda e, bk=bk, R=R: e.matmul(bk[:, 128:256], lhsT=ident, rhs=R, start=False, stop=True, **MM), [kW(d), 'cb'], [('ps', BK[d])], inc=False)
                    elif not lastm:
                        em.op('pe', lambda e, bk=bk, W=W, PT=PT: e.matmul(bk[:, 0:256], lhsT=PT, rhs=W[:, 128:384], start=True, stop=False, **MM), [kW(d)], [('ps', BK[d])], inc=False)
                        em.op('pe', lambda e, bk=bk, R=R: e.matmul(bk[:, 128:256], lhsT=ident, rhs=R, start=False, stop=True, **MM), [kW(d), 'cb'], [('ps', BK[d])], inc=False)
                    else:
                        em.op('pe', lambda e, bk=bk, R=R, PT=PT: e.matmul(bk[:, 128:256], lhsT=PT, rhs=R, start=True, stop=False, **MM), [kW(d)], [('ps', BK[d])], inc=False)
                        em.op('pe', lambda e, bk=bk, R=R: e.matmul(bk[:, 128:256], lhsT=ident, rhs=R, start=False, stop=True, **MM), [kW(d), 'cb'], [('ps', BK[d])])
                    if not lastm:
                        em.op('pe', lambda e, bk=bk, P=P, PT=PT: e.matmul(bk[:, 256:384], lhsT=P, rhs=PT, start=False, stop=True, **MM), [kW(d)], [('ps', BK[d])])
                yield
                for d in range(2):
                    W = Wb[d][p4]
                    eng = 'act' if (d == 0 or m % 2 == 1) else 'dve'
                    lo, hi = (128, 256) if lastm else (0, 384)
                    if eng == 'act':
                        em.op('act', lambda e, d=d, W=W: e.activation(out=W[:, 128 + lo:128 + hi], in_=ps[BK[d]][:, lo:hi], func=AF.Copy), [('ps', BK[d])], [kW(d)])
                    else:
                        em.op('dve', lambda e, d=d, W=W: e.tensor_copy(out=W[:, 128 + lo:128 + hi], in_=ps[BK[d]][:, lo:hi]), [('ps', BK[d])], [kW(d)])
                yield

        def chain_step(i):
            p3 = i % 4
            kW = lambda d: ('Wb', d, p3)
            blks = (i, NB - 1 - i)
            C_, D_ = (4, 5), (6, 7)
            for d in range(2):
                n = blks[d]
                em.op('pe', lambda e, d=d, n=n: e.matmul(ps[C_[d]][:, 0:128], lhsT=qk[:, n, 1, :], rhs=Sb[d][:], start=True, stop=True), ['qk', ('Sb', d)], [('ps', C_[d])], inc=False)
                em.op('pe', lambda e, d=d, n=n: e.matmul(ps[C_[d]][:, 128:256], lhsT=qk[:, n, 0, :], rhs=Sb[d][:], start=True, stop=True), ['qk', ('Sb', d)], [('ps', C_[d])])
                em.op('dve', lambda e, d=d, n=n: e.tensor_scalar(out=kd[d][:], in0=ktok[:, n, :], scalar1=gt[:, 12 + d, n:n + 1], scalar2=None, op0=OP.mult), ['ktok', 'gt'], [('kd', d)])
            yield
            for d in range(2):
                n = blks[d]
                em.op('dve', lambda e, d=d, n=n: e.scalar_tensor_tensor(out=rr[d][:], in0=ps[C_[d]][:, 0:128], scalar=gt[:, 8 + d, n:n + 1], in1=vtok[:, n, :], op0=OP.mult, op1=OP.add),
                      [('ps', C_[d]), 'gt', 'vtok'], [('rr', d)])
            yield
            for d in range(2):
                n = blks[d]
                em.op('pe', lambda e, d=d: e.matmul(ps[C_[d]][:, 256:384], lhsT=Wb[d][p3][:, 256:384], rhs=rr[d][:], start=True, stop=True), [kW(d), ('rr', d)], [('ps', C_[d])])
                em.op('dve', lambda e, d=d, n=n: e.tensor_scalar(out=ot[d][:], in0=ps[C_[d]][:, 128:256], scalar1=gt[:, 6 + d, n:n + 1], scalar2=None, op0=OP.mult), [('ps', C_[d]), 'gt'], [('ot', d)])
            yield
            if GDN_LEVEL < 9:
                return
            for d in range(2):
                n = blks[d]
                em.op('act', lambda e, d=d: e.activation(out=vn[d][:], in_=ps[C_[d]][:, 256:384], func=AF.Copy), [('ps', C_[d])], [('vn', d)])
            yield
            for d in range(2):
                em.op('pe', lambda e, d=d: e.matmul(ps[D_[d]][:, 0:128], lhsT=kd[d][:], rhs=vn[d][:], start=True, stop=True), [('kd', d), ('vn', d)], [('ps', D_[d])], inc=False)
                em.op('pe', lambda e, d=d: e.matmul(ps[D_[d]][:, 128:256], lhsT=Wb[d][p3][:, 0:128], rhs=vn[d][:], start=True, stop=True), [kW(d), ('vn', d)], [('ps', D_[d])])
            yield
            if GDN_LEVEL < 10:
                return
            for d in range(2):
                n = blks[d]
                em.op('dve', lambda e, d=d, n=n: e.scalar_tensor_tensor(out=Sb[d][:], in0=Sf[d][:], scalar=egl[:, d, n:n + 1], in1=ps[D_[d]][:, 0:128], op0=OP.mult, op1=OP.add),
                      [('S', d), 'egl', ('ps', D_[d])], [('Sb', d)])
            yield
            for d in range(2):
                n = blks[d]
                em.op('dve', lambda e, d=d, n=n: e.scalar_tensor_tensor(out=Sf[d][:], in0=Sf[d][:], scalar=egl[:, d, n:n + 1], in1=ps[D_[d]][:, 0:128], op0=OP.mult, op1=OP.add),
                      [('S', d), 'egl', ('ps', D_[d])], [('S', d)])
                if i < NB // 2:
                    em.op('dve', lambda e, d=d, n=n: e.tensor_tensor(out=oacc[:, n, :], in0=ps[D_[d]][:, 128:256], in1=ot[d][:], op=OP.add), [('ps', D_[d]), ('ot', d)], [('oacc', n)])
                else:
                    em.op('dve', lambda e, d=d, n=n: e.tensor_tensor(out=ot[d][:], in0=ps[D_[d]][:, 128:256], in1=ot[d][:], op=OP.add), [('ps', D_[d]), ('ot', d)], [('ot', d)])
                    em.op('pool', lambda e, d=d, n=n: e.tensor_tensor(out=oacc[:, n, :], in0=oacc[:, n, :], in1=ot[d][:], op=OP.add), [('oacc', n), ('ot', d)], [('oacc', n)])
            yield

        def fin_step(i):
            blks = (i, NB - 1 - i)
            D_ = (6, 7)
            if i >= NB // 2 and GDN_LEVEL >= 11:
                for n in blks:
                    mg = n % 2
                    em.op('dve', lambda e, n=n: e.tensor_tensor(out=osq[:], in0=oacc[:, n, :], in1=oacc[:, n, :], op=OP.mult), [('oacc', n)], ['osq'])
                    em.op('dve', lambda e: e.tensor_reduce(out=ssq[:, 0:1], in_=osq[:], axis=mybir.AxisListType.X, op=OP.add), ['osq', 'ssq'], ['ssq'])
                    rsqrt(ssq[:, 1:2], ssq[:, 0:1], 1.0 / 128, ['ssq'], ['ssq'], ssq[:, 1:2], 'ssq')
                    em.op('dve', lambda e, n=n: e.tensor_scalar(out=on[:], in0=oacc[:, n, :], scalar1=ssq[:, 1:2], scalar2=None, op0=OP.mult), [('oacc', n), 'ssq'], ['on'])
                    em.op('pe', lambda e: e.matmul(ps[D_[0]][:, 256:384], lhsT=on[:], rhs=ident, start=True, stop=True), ['on', 'cb'], [('ps', D_[0])])
                    yield
                    em.op('act', lambda e, n=n, mg=mg: e.activation(out=on2[:], in_=ps[D_[0]][:, 256:384], func=AF.Copy), [('ps', D_[0])], ['on2'])
                    em.op('dve', lambda e, n=n, mg=mg: e.scalar_tensor_tensor(out=mixg[mg][:], in0=on2[:], scalar=onw_s[:, 0:1], in1=zs[:, n * 128:(n + 1) * 128], op0=OP.mult, op1=OP.mult),
                          ['on2', 'onw', 'zs'], [('mixg', mg)])
                    em.dma(mix_loc[h * 128:(h + 1) * 128, n * 128:(n + 1) * 128], mixg[mg][:], ('mixg', mg), [('mixg', mg)], [])
                    yield

        def run_interleaved(gens):
            gens = [g for g in gens if g is not None]
            while gens:
                for g in list(gens):
                    try:
                        next(g)
                    except StopIteration:
                        gens.remove(g)

        def seq(*gs):
            for g in gs:
                yield from g

        for j in range(NB // 2 + 1):
            tasks = []
            if j >= 1 and GDN_LEVEL >= 8:
                tasks.append(seq(chain_step(2 * j - 2), chain_step(2 * j - 1)))
            if j >= 2 and 2 * j - 4 >= NB // 2:
                tasks.append(seq(fin_step(2 * j - 4), fin_step(2 * j - 3)))
            if j < NB // 2:
                tasks += [pre_step(2 * j), pre_step(2 * j + 1)]
            run_interleaved(tasks)
        run_interleaved([seq(fin_step(NB - 2), fin_step(NB - 1))])

    stk['cur'].close()
    stk['cur'] = ExitStack()
    em.barrier()
    em.dma(mix_d[bass.ds(rank * 256, 256), :], mix_loc[0:256, :], 'xch', [], [], q='pool')
    em.dma(mix_d[bass.ds(rank * 256 + 512, 256), :], mix_loc[256:512, :], 'xch', [], [], q='pool')
    em.barrier()
    nc.all_core_barrier()
    em.dma(mix_in, mix_d[:, bass.ds(rank * TH, TH)], 'xch', [], [], q='pool')
    em.barrier()
    wo = sb("wo", [128, KC, D], BF)
    alloc_xs(2, 'o')
    xs = XS['b']
    for half in range(2):
        s = xs[half]
        em.dma(s[:], wout.rearrange("(kc p) n -> p kc n", p=128)[:, :, half * 512:(half + 1) * 512], ('xs', half), [], [('xs', half)])
        em.op('dve', lambda e: e.tensor_copy(out=wo[:, :, half * 512:(half + 1) * 512], in_=s[:]), [('xs', half)], ['wo'])
    mt = [sb("mt%d" % i, [128, KC, 512], BF) for i in range(2)]
    xr = [sb("xr%d" % i, [128, D], F32) for i in range(4)]
    mixv = mix_in.rearrange("(kc p) n -> p kc n", p=128)
    for n in range(TH // 128 if 'out' in STAGES else 0):
        t4, sub = n // 4, n % 4
        ms = t4 % 2
        sl = n % 4
        tk = slice(n * 128, (n + 1) * 128)
        if sub == 0:
            em.dma(mt[ms][:], mixv[:, :, t4 * 512:(t4 + 1) * 512], ('mt', ms), [], [('mt', ms)])
        em.dma(xr[sl][:], xtok[tk, :], ('xr', sl), [], [('xr', sl)])
        for half in range(2):
            bank = half + 2 * (n % 2)
            for kc in range(KC):
                em.op('pe', lambda e, kc=kc: e.matmul(ps[bank][:, :], lhsT=mt[ms][:, kc, sub * 128:(sub + 1) * 128], rhs=wo[:, kc, half * 512:(half + 1) * 512], start=(kc == 0), stop=(kc == KC - 1)),
                      [('mt', ms), 'wo'], [('ps', bank)], inc=(kc == KC - 1))
            em.op('dve', lambda e: e.tensor_tensor(out=xr[sl][:, half * 512:(half + 1) * 512], in0=ps[bank][:, :], in1=xr[sl][:, half * 512:(half + 1) * 512], op=OP.add),
                  [('ps', bank), ('xr', sl)], [('xr', sl)])
        em.dma(y[tk, :], xr[sl][:], ('xr', sl), [('xr', sl)], [])
    em.barrier()
    stk['cur'].close()
    return nc, es


def _consts():
    j = np.arange(128)[:, None]
    c = np.arange(128)[None, :]
    m = np.zeros((128, 12, 128), np.float32)
    m[:, 0] = (j <= c)
    m[:, 1] = (j >= c)
    m[:, 2] = (j > c)
    m[:, 3] = (j < c)
    m[:, 4] = (c >= j)
    m[:, 5] = (c <= j)
    m[:, 6] = (c > j)
    m[:, 7] = (c < j)
    for q_ in range(4):
        m[:, 8 + q_] = np.where(m[:, 4 + q_] > 0, 0.0, -30000.0)
    b = np.zeros((128, 6, 128), np.float32)
    b[:, 0] = np.eye(128)
    b[:, 1] = 1.0
    b[:, 2] = ((j // 64) == (c // 64))
    b[:, 3] = (j >= c)
    b[:, 4] = (j <= c)
    b[:, 5] = 1.0
    inv = 10000.0 ** (-np.arange(0, 64, 2, dtype=np.float32) / 64)
    ang = np.arange(T, dtype=np.float32)[:, None] * inv[None, :]
    ang = np.concatenate([ang, ang], -1)
    cos = np.cos(ang).T.astype(np.float32)
    sin = np.sin(ang).T.astype(np.float32)
    sin_s = np.concatenate([-sin[:32], sin[32:]], 0)
    rope = np.zeros((128, 2, T), np.float32)
    rope[:, 0] = np.concatenate([cos, cos], 0)
    rope[:, 1] = np.concatenate([sin_s, sin_s], 0)
    return m, b.astype(ml_dtypes.bfloat16), rope


_CACHE = {}


def kernel(x, norm_w, w_in, dn_conv_w, dn_a_log, dn_dt_bias, dn_out_norm_w,
           swa_q_norm_w, swa_k_norm_w, swa_sinks, w_out):
    x = np.asarray(x, np.float32)
    w = np.asarray(w_in, np.float32)[0]
    if 'nc' not in _CACHE:
        _CACHE['nc'] = build_program()
    nc, _ = _CACHE['nc']
    cm, cbf, rope = _consts()
    perm = np.concatenate([np.arange(32, 64), np.arange(0, 32)])
    o = 2064
    wq = w[:, o:o + 512]
    wk = w[:, o + 512:o + 640]
    wv = w[:, o + 640:o + 768]
    wz = w[:, o + 768:o + 1280]
    cwT = np.asarray(dn_conv_w, np.float32)[0].T.reshape(12, 128, 5)
    alog = np.asarray(dn_a_log, np.float32)[0]
    dtb = np.asarray(dn_dt_bias, np.float32)[0]
    qw = np.asarray(swa_q_norm_w, np.float32)[0]
    kw = np.asarray(swa_k_norm_w, np.float32)[0]
    qkw = np.stack([np.tile(qw, 2), np.tile(qw[perm], 2), np.tile(kw, 2), np.tile(kw[perm], 2)], 1).astype(np.float32)
    sk = np.asarray(swa_sinks, np.float32)[0]
    normw = np.ascontiguousarray(np.asarray(norm_w, np.float32)[0].reshape(KC, 128).T)
    onw = np.asarray(dn_out_norm_w, np.float32)[0].reshape(128, 1)
    per_rank = []
    for r in range(2):
        wg = np.zeros((2, D, 516), np.float32)
        convw = np.zeros((128, 6, 5), np.float32)
        gpar = np.zeros((128, 8), np.float32)
        for hh in range(2):
            h = 2 * r + hh
            for g in range(4):
                wg[hh, :, g * 128:(g + 1) * 128] = w[:, g * 512 + h * 128: g * 512 + (h + 1) * 128]
            wg[hh, :, 512] = w[:, 2048 + h]
            wg[hh, :, 513] = w[:, 2048 + 4 + h]
            wg[hh, :, 514] = w[:, 2056 + h]
            wg[hh, :, 515] = w[:, 2056 + 4 + h]
            for c3 in range(3):
                convw[:, c3 * 2 + hh, :] = cwT[c3 * 4 + h]
            for d in range(2):
                gpar[:, d * 2 + hh] = alog[d, h]
                gpar[:, 4 + d * 2 + hh] = dtb[d, h]
        kk = wk[:, r * 64:(r + 1) * 64]
        wswk = np.concatenate([kk, kk, kk[:, perm], kk[:, perm], wv[:, r * 64:(r + 1) * 64]], 1)
        wq_r = wq[:, r * 256:(r + 1) * 256]
        wswq = np.concatenate([wq_r, wq_r.reshape(D, 4, 64)[:, :, perm].reshape(D, 256), wz[:, r * 256:(r + 1) * 256]], 1)
        sinks = np.zeros((128, 2), np.float32)
        for cc in range(2):
            sinks[0:64, cc] = sk[2 * (2 * r + cc)]
            sinks[64:128, cc] = sk[2 * (2 * r + cc) + 1]
        per_rank.append(dict(wgdn=wg, convw=convw, gpar=gpar, wswk=np.ascontiguousarray(wswk), wswq=np.ascontiguousarray(wswq), sinks=sinks))
    common = dict(normw=normw, wout=np.asarray(w_out, np.float32)[0], onw=onw, qkw=qkw, cmask=cm, cbf=cbf, rope=rope)
    in_maps = []
    for c in range(NCORES):
        b, r = c // 2, c % 2
        m = dict(common)
        m.update(per_rank[r])
        m['xTh'] = np.ascontiguousarray(x[b, r * TH:(r + 1) * TH].T)
        m['xtok'] = np.ascontiguousarray(x[b, r * TH:(r + 1) * TH])
        in_maps.append(m)
    res = run_bass_kernel_spmd(nc, in_maps, core_ids=list(range(NCORES)))
    _CACHE['res'] = res
    out = np.zeros((2, T, D), np.float32)
    for c in range(NCORES):
        b, r = c // 2, c % 2
        out[b, r * TH:(r + 1) * TH] = np.asarray(res.results[c]['y'], np.float32)
    return out
```
